# Optimizing a Trainium2 kernel written in Bass

```python
import math
import jax, jax.numpy as jnp
from jax import lax
import numpy as np

D_MODEL = 1024
BATCH = 16
SEQ = 4096
DEPTH = 1

PLE_DIM = 256
D_FF = 2816

GDN_HEADS = 4
GDN_DK = 128
GDN_DV = 128
GDN_CONV = 4
GDN_CHUNK = 64

DIFF_HEADS = 4
DIFF_DQK = 64
DIFF_DV = 2 * DIFF_DQK
ROPE_THETA = 500000.0
ROPE_DIM = DIFF_DQK // 4
Q_BLOCK = 128

LN_EPS = 1e-5
RMS_EPS = 1e-6

GDN_QK_W = GDN_HEADS * GDN_DK
GDN_V_W = GDN_HEADS * GDN_DV
DIFF_QK_W = DIFF_HEADS * 2 * DIFF_DQK
DIFF_V_W = DIFF_HEADS * DIFF_DV
MIX_WIDTH = GDN_V_W + DIFF_V_W
IN_SECTIONS = (GDN_QK_W, GDN_QK_W, GDN_V_W, GDN_V_W, GDN_HEADS, GDN_HEADS,
               DIFF_QK_W, DIFF_QK_W, DIFF_V_W)
IN_WIDTH = sum(IN_SECTIONS)

DEEPNORM_ALPHA = (2.0 * DEPTH) ** 0.25
DEEPNORM_BETA = (8.0 * DEPTH) ** -0.25

kernel_name = "hybrid_gdn_diffattn_macaron_deepnorm"


def layer_norm(x, g, b):
    xf = x.astype(jnp.float32)
    mu = jnp.mean(xf, axis=-1, keepdims=True)
    var = jnp.mean(jnp.square(xf - mu), axis=-1, keepdims=True)
    return ((xf - mu) * lax.rsqrt(var + LN_EPS) * g.astype(jnp.float32)
            + b.astype(jnp.float32)).astype(x.dtype)


def rms_norm(x, w):
    xf = x.astype(jnp.float32)
    y = xf * lax.rsqrt(jnp.mean(xf * xf, axis=-1, keepdims=True) + RMS_EPS)
    return (y * w.astype(jnp.float32)).astype(x.dtype)


def l2_normalize(x):
    xf = x.astype(jnp.float32)
    return xf * lax.rsqrt(jnp.sum(xf * xf, axis=-1, keepdims=True) + RMS_EPS)


def swiglu_ffn(x, w13, w2):
    gate, up = jnp.split(x @ w13, 2, axis=-1)
    return (jax.nn.silu(gate) * up) @ w2


def rotary_tables(seq_len):
    inv_freq = ROPE_THETA ** (-jnp.arange(0, ROPE_DIM, 2, dtype=jnp.float32) / ROPE_DIM)
    ang = jnp.arange(seq_len, dtype=jnp.float32)[:, None] * inv_freq[None, :]
    return jnp.cos(ang), jnp.sin(ang)


def partial_rotary(x, cos, sin):
    half = ROPE_DIM // 2
    x1, x2, rest = x[..., :half], x[..., half:ROPE_DIM], x[..., ROPE_DIM:]
    c = cos[:, None, None, :]
    s = sin[:, None, None, :]
    rot = jnp.concatenate([x1 * c - x2 * s, x2 * c + x1 * s], axis=-1).astype(x.dtype)
    return jnp.concatenate([rot, rest], axis=-1)


def causal_depthwise_conv(x, w):
    k_width, ch = w.shape
    return lax.conv_general_dilated(
        x, w[:, None, :].astype(x.dtype), window_strides=(1,), padding=[(k_width - 1, 0)],
        dimension_numbers=("NWC", "WIO", "NWC"), feature_group_count=ch)


def gated_delta_rule_chunked(q, k, v, g, beta):
    bsz, seq, nh, dk = q.shape
    dv = v.shape[-1]
    c = GDN_CHUNK
    n = seq // c

    def chunks(t):
        t = t.reshape((bsz, n, c, nh) + t.shape[3:])
        return jnp.moveaxis(t, 3, 1)

    q = chunks(q) * (dk ** -0.5)
    k = chunks(k)
    v = chunks(v)
    beta = chunks(beta)
    g = jnp.cumsum(chunks(g), axis=-1)

    causal = jnp.tril(jnp.ones((c, c), dtype=bool))
    strict = jnp.tril(jnp.ones((c, c), dtype=bool), -1)
    decay = jnp.exp(jnp.where(causal, g[..., :, None] - g[..., None, :], -jnp.inf))

    kk = jnp.einsum("bhncd,bhnsd->bhncs", k, k)
    lower = jnp.where(strict, kk * decay * beta[..., None], 0.0)
    eye = jnp.eye(c, dtype=jnp.float32)
    t_mat = lax.linalg.triangular_solve(eye + lower, jnp.broadcast_to(eye, lower.shape),
                                        left_side=True, lower=True, unit_diagonal=True)
    u = jnp.einsum("bhncs,bhnsv->bhncv", t_mat, v * beta[..., None])
    w = jnp.einsum("bhncs,bhnsd->bhncd", t_mat, k * (beta * jnp.exp(g))[..., None])

    qk = jnp.einsum("bhncd,bhnsd->bhncs", q, k) * decay
    q_dec = q * jnp.exp(g)[..., None]
    g_last = g[..., -1]
    k_dec = k * jnp.exp(g_last[..., None] - g)[..., None]

    def step(state, xs):
        u_n, w_n, qk_n, qdec_n, kdec_n, glast_n = xs
        v_new = u_n - jnp.einsum("bhck,bhkv->bhcv", w_n, state)
        o = (jnp.einsum("bhck,bhkv->bhcv", qdec_n, state)
             + jnp.einsum("bhcs,bhsv->bhcv", qk_n, v_new))
        state = (state * jnp.exp(glast_n)[..., None, None]
                 + jnp.einsum("bhck,bhcv->bhkv", kdec_n, v_new))
        return state, o

    xs = tuple(jnp.moveaxis(t, 2, 0) for t in (u, w, qk, q_dec, k_dec, g_last))
    state0 = jnp.zeros((bsz, nh, dk, dv), dtype=jnp.float32)
    _, o = lax.scan(step, state0, xs)
    o = jnp.moveaxis(o, 0, 2)
    return jnp.moveaxis(o, 1, 3).reshape(bsz, seq, nh, dv)


def gdn_mixer(q, k, v, z, a, b, conv_w, a_log, dt_bias, norm_w):
    bsz, seq, _ = q.shape
    qkv = jax.nn.silu(causal_depthwise_conv(jnp.concatenate([q, k, v], axis=-1), conv_w))
    q, k, v = jnp.split(qkv, [GDN_QK_W, 2 * GDN_QK_W], axis=-1)
    q = l2_normalize(q.reshape(bsz, seq, GDN_HEADS, GDN_DK))
    k = l2_normalize(k.reshape(bsz, seq, GDN_HEADS, GDN_DK))
    v = v.reshape(bsz, seq, GDN_HEADS, GDN_DV).astype(jnp.float32)
    g = -jnp.exp(a_log.astype(jnp.float32)) * jax.nn.softplus(
        a.astype(jnp.float32) + dt_bias.astype(jnp.float32))
    beta = jax.nn.sigmoid(b.astype(jnp.float32))
    o = gated_delta_rule_chunked(q, k, v, g, beta)
    gate = jax.nn.silu(z.reshape(bsz, seq, GDN_HEADS, GDN_DV).astype(jnp.float32))
    o = rms_norm(o, norm_w) * gate
    return o.reshape(bsz, seq, GDN_V_W).astype(z.dtype)


def diff_attention_mixer(q, k, v, lq1, lk1, lq2, lk2, subln_w, lam_init, cos, sin):
    bsz, seq, _ = q.shape
    q = partial_rotary(q.reshape(bsz, seq, DIFF_HEADS, 2, DIFF_DQK), cos, sin)
    k = partial_rotary(k.reshape(bsz, seq, DIFF_HEADS, 2, DIFF_DQK), cos, sin)
    v = v.reshape(bsz, seq, DIFF_HEADS, DIFF_DV)
    f32 = jnp.float32
    lam = (jnp.exp(jnp.sum(lq1.astype(f32) * lk1.astype(f32)))
           - jnp.exp(jnp.sum(lq2.astype(f32) * lk2.astype(f32))) + lam_init)
    nb = seq // Q_BLOCK
    q_blocks = jnp.moveaxis(q.reshape(bsz, nb, Q_BLOCK, DIFF_HEADS, 2, DIFF_DQK), 1, 0)
    pos = jnp.arange(seq, dtype=jnp.int32)
    q_pos = pos.reshape(nb, Q_BLOCK)
    scale = DIFF_DQK ** -0.5

    def block(args):
        qb, qp = args
        s = jnp.einsum("bqhmd,bkhmd->bhmqk", qb, k).astype(f32) * scale
        s = jnp.where(qp[:, None] >= pos[None, :], s, -jnp.inf)
        pr = jax.nn.softmax(s, axis=-1)
        attn = pr[:, :, 0] - lam * pr[:, :, 1]
        return jnp.einsum("bhqk,bkhd->bqhd", attn.astype(v.dtype), v)

    o = lax.map(block, (q_blocks, q_pos))
    o = jnp.moveaxis(o, 0, 1).reshape(bsz, seq, DIFF_HEADS, DIFF_DV)
    o = rms_norm(o, subln_w) * (1.0 - lam_init)
    return o.reshape(bsz, seq, DIFF_V_W)


def setup_inputs(seed: int = 0) -> dict:
    key = jax.random.key(seed)
    ks = jax.random.split(key, 32)
    L, D, F = DEPTH, D_MODEL, D_FF
    nrm = lambda k, shape, s: jax.random.normal(k, shape, dtype=jnp.float32) * s
    gain = lambda k, shape: 1.0 + 0.02 * jax.random.normal(k, shape, dtype=jnp.float32)
    dt = jnp.exp(jax.random.uniform(ks[14], (L, GDN_HEADS), minval=math.log(1e-3), maxval=math.log(1e-1)))
    return {
        "x": nrm(ks[0], (BATCH, SEQ, D), 1.0),
        "p": nrm(ks[1], (DEPTH, BATCH, SEQ, PLE_DIM), 1.0),
        "ffn1_w13": nrm(ks[2], (L, D, 2 * F), D ** -0.5),
        "ffn1_w2": nrm(ks[3], (L, F, D), F ** -0.5 * DEEPNORM_BETA),
        "ln1_g": gain(ks[4], (L, D)),
        "ln1_b": nrm(ks[5], (L, D), 0.02),
        "w_in": nrm(ks[6], (L, D, IN_WIDTH), D ** -0.5),
        "gdn_conv_w": nrm(ks[7], (L, GDN_CONV, 2 * GDN_QK_W + GDN_V_W), GDN_CONV ** -0.5),
        "gdn_a_log": jnp.log(jax.random.uniform(ks[8], (L, GDN_HEADS), minval=1.0, maxval=16.0)),
        "gdn_dt_bias": dt + jnp.log(-jnp.expm1(-dt)),
        "gdn_norm_w": gain(ks[9], (L, GDN_DV)),
        "diff_lq1": nrm(ks[10], (L, DIFF_DQK), 0.1),
        "diff_lk1": nrm(ks[11], (L, DIFF_DQK), 0.1),
        "diff_lq2": nrm(ks[12], (L, DIFF_DQK), 0.1),
        "diff_lk2": nrm(ks[13], (L, DIFF_DQK), 0.1),
        "diff_subln_w": gain(ks[15], (L, DIFF_DV)),
        "w_out": nrm(ks[16], (L, MIX_WIDTH, D), MIX_WIDTH ** -0.5 * DEEPNORM_BETA),
        "ln2_g": gain(ks[17], (L, D)),
        "ln2_b": nrm(ks[18], (L, D), 0.02),
        "ffn2_w13": nrm(ks[19], (L, D, 2 * F), D ** -0.5),
        "ffn2_w2": nrm(ks[20], (L, F, D), F ** -0.5 * DEEPNORM_BETA),
        "ple_gate_w": nrm(ks[21], (L, D, D), D ** -0.5),
        "ple_proj_w": nrm(ks[22], (L, PLE_DIM, D), PLE_DIM ** -0.5 * DEEPNORM_BETA),
        "ln3_g": gain(ks[23], (L, D)),
        "ln3_b": nrm(ks[24], (L, D), 0.02),
    }


def reference(x, p, ffn1_w13, ffn1_w2, ln1_g, ln1_b, w_in, gdn_conv_w, gdn_a_log, gdn_dt_bias,
              gdn_norm_w, diff_lq1, diff_lk1, diff_lq2, diff_lk2, diff_subln_w, w_out, ln2_g, ln2_b,
              ffn2_w13, ffn2_w2, ple_gate_w, ple_proj_w, ln3_g, ln3_b):
    cos, sin = rotary_tables(x.shape[1])
    offsets = [int(o) for o in np.cumsum(IN_SECTIONS)[:-1]]
    alpha = DEEPNORM_ALPHA
    for i in range(DEPTH):
        lam_init = 0.8 - 0.6 * math.exp(-0.3 * i)
        x = layer_norm(alpha * x + 0.5 * swiglu_ffn(x, ffn1_w13[i], ffn1_w2[i]), ln1_g[i], ln1_b[i])
        h = x @ w_in[i]
        gq, gk, gv, gz, ga, gb, dq, dk, dv = jnp.split(h, offsets, axis=-1)
        y_a = gdn_mixer(gq, gk, gv, gz, ga, gb, gdn_conv_w[i], gdn_a_log[i], gdn_dt_bias[i],
                        gdn_norm_w[i])
        y_b = diff_attention_mixer(dq, dk, dv, diff_lq1[i], diff_lk1[i], diff_lq2[i], diff_lk2[i],
                                   diff_subln_w[i], lam_init, cos, sin)
        mix = jnp.concatenate([y_a, y_b], axis=-1)
        x = layer_norm(alpha * x + mix @ w_out[i], ln2_g[i], ln2_b[i])
        ple = jax.nn.sigmoid(x @ ple_gate_w[i]) * (p[i] @ ple_proj_w[i])
        x = layer_norm(alpha * x + 0.5 * swiglu_ffn(x, ffn2_w13[i], ffn2_w2[i]) + ple,
                       ln3_g[i], ln3_b[i])
    return x
```

```python
import math
from contextlib import ExitStack

import numpy as np
import concourse.bass as bass
import concourse.mybir as mybir
from concourse.bass_utils import run_bass_kernel_spmd

F32 = mybir.dt.float32
BF16 = mybir.dt.bfloat16
AF = mybir.ActivationFunctionType
ALU = mybir.AluOpType

D = 1024
FF = 2816
NJ = FF // 128
PLE = 256
INW = 3592
ALPHA = 2.0 ** 0.25
LN_EPS = 1e-5
RMS_EPS = 1e-6
LAM_INIT = 0.8 - 0.6 * math.exp(-0.3 * 0)

SEM_LIMIT = 30000


class Buf:
    __slots__ = ("name", "last_w", "readers", "dsem", "dcount")

    def __init__(self, name):
        self.name = name
        self.last_w = None
        self.readers = []
        self.dsem = None
        self.dcount = 0


class Eng:
    def __init__(self, name):
        self.name = name
        self.items = []
        self.semkey = None
        self.count = 0
        self.known = {}


class Prog:
    def __init__(self, nc, stack):
        self.nc = nc
        self.stack = stack
        self.sems = {}
        self.nsem = 0
        self.free_dsems = []
        self.dma_bufs = []
        self.engs = {n: Eng(n) for n in ("pe", "act", "dve", "pool", "sp")}
        for e in self.engs.values():
            e.semkey = self.new_sem(e.name)

    def new_sem(self, name):
        key = "%s_%d" % (name, self.nsem)
        self.nsem += 1
        self.sems[key] = self.stack.enter_context(self.nc.semaphore(key))
        return key

    def _need(self, eng, toks):
        for t in toks:
            if t is None:
                continue
            k, v = t
            if eng.known.get(k, 0) >= v:
                continue
            if k == eng.semkey and v > eng.count:
                continue
            eng.known[k] = v
            eng.items.append(("wait", k, v))

    def op(self, engname, fn, reads=(), writes=(), inc=True):
        eng = self.engs[engname]
        toks = []
        for b in reads:
            toks.append(b.last_w)
        for b in writes:
            toks.append(b.last_w)
            toks.extend(b.readers)
        self._need(eng, toks)
        if eng.count >= SEM_LIMIT and inc:
            eng.semkey = self.new_sem(eng.name)
            eng.count = 0
        tok = (eng.semkey, eng.count + 1)
        if inc:
            eng.count += 1
            if engname == "pe":
                eng.known[eng.semkey] = eng.count
        eng.items.append(("op", fn, eng.semkey if inc else None, 1))
        for b in reads:
            b.readers.append(tok)
        for b in writes:
            b.last_w = tok
            b.readers = []
        return tok

    def dma(self, qname, out_ap, in_ap, reads, writes, sbuf, **kw):
        eng = self.engs[qname]
        toks = []
        for b in reads:
            toks.append(b.last_w)
        for b in writes:
            toks.append(b.last_w)
            toks.extend(b.readers)
        self._need(eng, toks)
        if qname == "pool":
            assert sbuf.dsem is None
            sbuf.dsem, sbuf.dcount = self.new_sem("w"), 0
        elif sbuf.dsem is None:
            if self.free_dsems:
                sbuf.dsem, sbuf.dcount = self.free_dsems.pop()
            else:
                sbuf.dsem, sbuf.dcount = self.new_sem("d"), 0
            self.dma_bufs.append(sbuf)
        sbuf.dcount += 16
        tok = (sbuf.dsem, sbuf.dcount)

        def fn(e, out_ap=out_ap, in_ap=in_ap, kw=kw):
            return e.dma_start(out=out_ap, in_=in_ap, **kw)

        eng.items.append(("op", fn, sbuf.dsem, 16))
        for b in reads:
            b.readers.append(tok)
        for b in writes:
            b.last_w = tok
            b.readers = []
        return tok

    def wait_all(self, engname, bufs):
        eng = self.engs[engname]
        toks = []
        for b in bufs:
            toks.append(b.last_w)
            toks.extend(b.readers)
        self._need(eng, toks)

    def emit(self):
        nc = self.nc
        with nc.Block() as block:
            def runner(eng):
                def body(e):
                    for it in eng.items:
                        if it[0] == "wait":
                            e.wait_ge(self.sems[it[1]], it[2])
                        else:
                            ins = it[1](e)
                            if it[2] is not None:
                                ins.then_inc(self.sems[it[2]], it[3])
                return body
            block.sync(runner(self.engs["sp"]))
            block.scalar(runner(self.engs["act"]))
            block.vector(runner(self.engs["dve"]))
            block.gpsimd(runner(self.engs["pool"]))
            block.tensor(runner(self.engs["pe"]))
        for e in self.engs.values():
            e.items = []
        for b in self.dma_bufs:
            self.free_dsems.append((b.dsem, b.dcount))
            b.dsem = None
        self.dma_bufs = []


class Builder:
    def __init__(self, nc, stack, cfg):
        self.nc = nc
        self.stack = stack
        self.cfg = cfg
        self.P = Prog(nc, stack)
        self.nbuf = 0

    def sb(self, name, shape, dt):
        t = self.stack.enter_context(self.nc.sbuf_tensor("sb_" + name, list(shape), dt))
        return t, Buf(name)

    def sbn(self, name, shape, dt, n):
        return [self.sb("%s%d" % (name, i), shape, dt) for i in range(n)]

    def ps(self, name, shape, dt):
        t = self.stack.enter_context(self.nc.psum_tensor("ps_" + name, list(shape), dt))
        return t, Buf(name)


    def mm(self, out, ob, lhsT, rhs, reads, start=True, stop=True):
        self.P.op("pe", lambda e: e.matmul(out, lhsT=lhsT, rhs=rhs, start=start, stop=stop),
                  reads=reads, writes=[ob], inc=stop)

    def tr(self, out, ob, in_, reads, inc=True):
        idn = self.ident
        self.P.op("pe", lambda e: e.transpose(out=out, in_=in_, identity=idn[0][:]),
                  reads=list(reads) + [idn[1]], writes=[ob], inc=inc)

    def act(self, out, in_, func, reads, writes, **kw):
        self.P.op("act", lambda e: e.activation(out=out, in_=in_, func=func, **kw), reads=reads, writes=writes)

    def tt(self, eng, out, in0, in1, op, reads, writes):
        self.P.op(eng, lambda e: e.tensor_tensor(out=out, in0=in0, in1=in1, op=op), reads=reads, writes=writes)

    def ts(self, eng, out, in0, s1, s2, op0, op1, reads, writes):
        self.P.op(eng, lambda e: e.tensor_scalar(out=out, in0=in0, scalar1=s1, scalar2=s2, op0=op0, op1=op1),
                  reads=reads, writes=writes)

    def stt(self, out, in0, scalar, in1, op0, op1, reads, writes):
        self.P.op("dve", lambda e: e.scalar_tensor_tensor(out=out, in0=in0, scalar=scalar, in1=in1, op0=op0, op1=op1),
                  reads=reads, writes=writes)

    def cp(self, eng, out, in_, reads, writes):
        if eng == "act":
            self.P.op("act", lambda e: e.copy(out=out, in_=in_), reads=reads, writes=writes)
        else:
            self.P.op(eng, lambda e: e.tensor_copy(out=out, in_=in_), reads=reads, writes=writes)

    def banks(self, st, tag):
        self.pbanks = [self.ps_in(st, "%sbk%d" % (tag, i), [128, 512], F32) for i in range(8)]
        self.bki = 0
        self.rotbanks = None

    def ps_in(self, st, name, shape, dt):
        t = st.enter_context(self.nc.psum_tensor("ps_" + name, list(shape), dt))
        return t, Buf(name)

    def bank(self):
        rot = getattr(self, "rotbanks", None) or list(range(8))
        b = self.pbanks[rot[self.bki % len(rot)]]
        self.bki += 1
        return b

    def dram(self, name, shape, dt, kind):
        return self.nc.dram_tensor(name, list(shape), dt, kind=kind).ap()


def ffn_phase(B, x_src, out_dst, w13_d, w2_d, lng_d, lnb_d, ident_d, ntok, G=512, ple=None, tag="f"):
    nc, P = B.nc, B.P
    TPG = G // 128
    ngroups = ntok // G
    with ExitStack() as st:
        B.stack = st
        NR = 6
        w13 = B.sb(tag + "w13", [128, 8, 2 * FF], BF16)
        w2 = B.sb(tag + "w2", [128, NJ, D], BF16)
        gB = B.sb(tag + "gB", [128, D], F32)
        bB = B.sb(tag + "bB", [128, D], F32)
        ident = B.sb(tag + "ident", [128, 128], F32)
        B.ident = (ident[0][:], ident[1])
        r = B.sbn(tag + "r", [128, D], F32, NR)
        xT = B.sb(tag + "xT", [128, 8, G], BF16)
        hT = B.sb(tag + "hT", [128, NJ, G], BF16)
        xTb = [Buf(tag + "xT%d" % c) for c in range(8)]
        hTb = [Buf(tag + "hT%d" % c) for c in range(NJ)]
        sl = B.sbn(tag + "s", [128, G], F32, 2)
        stt = B.sbn(tag + "st", [128, 2, 6], F32, 2)
        mv = B.sbn(tag + "mv", [128, 2], F32, 2)
        sd = B.sbn(tag + "sd", [128, 1], F32, 2)
        rs = B.sbn(tag + "rs", [128, 1], F32, 2)
        pT = [B.ps(tag + "pT%d" % i, [128, 512], F32) for i in range(2)]
        pg = [B.ps(tag + "pg%d" % i, [128, 512], F32) for i in range(2)]
        pu = [B.ps(tag + "pu%d" % i, [128, 512], F32) for i in range(2)]
        pd = [B.ps(tag + "pd%d" % i, [128, 512], F32) for i in range(2)]
        extra = []
        if ple is not None:
            plebuf = B.sbn(tag + "ple", [128, D], F32, 2)
            extra = plebuf
        P.dma("sp", ident[0][:], ident_d, [], [ident[1]], ident[1])
        P.dma("sp", gB[0][:], lng_d, [], [gB[1]], gB[1])
        P.dma("sp", bB[0][:], lnb_d, [], [bB[1]], bB[1])
        w13v = w13_d.rearrange("(c p) f -> p c f", p=128)
        w2v = w2_d.rearrange("(c p) f -> p c f", p=128)
        w13bufs = [Buf(tag + "w13_%d" % c) for c in range(8)]
        for c in range(8):
            P.dma("pool", w13[0][:, c, :], w13v[:, c, :], [], [w13bufs[c]], w13bufs[c], max_dma_last_dim=8192)
        wgb = []
        w2p = [Buf(tag + "w2_%d" % c) for c in range(3)]
        w2bufs = [w2p[min(c // 4, 2)] for c in range(NJ // 2)]
        for c, (a, b) in enumerate(((0, 8), (8, 16), (16, NJ))):
            P.dma("pool", w2[0][:, a:b, :], w2v[:, a:b, :], [], [w2p[c]], w2p[c], max_dma_last_dim=8192)

        ti = 0
        for g in range(ngroups):
            slots = []
            for tt in range(TPG):
                rt, rb = r[ti % NR]
                ti += 1
                slots.append((rt, rb))
                t0 = g * G + tt * 128
                P.dma("sp", rt[:], x_src[t0:t0 + 128, :], [], [rb], rb)
            for c in range(8):
                pt, pb = pT[c % 2]
                for tt in range(TPG):
                    rt, rb = slots[tt]
                    B.tr(pt[:, tt * 128:(tt + 1) * 128], pb, rt[:, c * 128:(c + 1) * 128], [rb], inc=(tt == TPG - 1))
                B.cp("act" if c % 2 == 0 else "dve", xT[0][:, c, :], pt[:, 0:G], [pb], [xTb[c]])
            for j in range(NJ):
                pgt, pgb = pg[j % 2]
                put, pub = pu[j % 2]
                for c in range(8):
                    B.mm(pgt[:, 0:G], pgb, w13[0][:, c, j * 128:(j + 1) * 128], xT[0][:, c, :], [w13bufs[c], xTb[c]], start=(c == 0), stop=(c == 7))
                for c in range(8):
                    B.mm(put[:, 0:G], pub, w13[0][:, c, FF + j * 128:FF + (j + 1) * 128], xT[0][:, c, :], [w13bufs[c], xTb[c]], start=(c == 0), stop=(c == 7))
                st_, sb_ = sl[j % 2]
                B.act(st_[:], pgt[:, 0:G], AF.Silu, [pgb], [sb_])
                B.stt(hT[0][:, j, :], st_[:], 0.5, put[:, 0:G], ALU.mult, ALU.mult, [sb_, pub], [hTb[j]])
            for tt in range(TPG):
                rt, rb = slots[tt]
                for dh in range(2):
                    pdt, pdb = pd[(tt * 2 + dh) % 2]
                    for j in range(NJ):
                        B.mm(pdt[:], pdb, hT[0][:, j, tt * 128:(tt + 1) * 128], w2[0][:, j, dh * 512:(dh + 1) * 512],
                             [hTb[j], w2bufs[j // 2]], start=(j == 0), stop=(j == NJ - 1))
                    B.stt(rt[:, dh * 512:(dh + 1) * 512], rt[:, dh * 512:(dh + 1) * 512], ALPHA, pdt[:], ALU.mult, ALU.add, [pdb, rb], [rb])
                if ple is not None:
                    plt, plb = plebuf[(g * TPG + tt) % 2]
                    t0 = g * G + tt * 128
                    P.dma("sp", plt[:], ple[t0:t0 + 128, :], [], [plb], plb)
                    B.tt("pool", rt[:], rt[:], plt[:], ALU.add, [rb, plb], [rb])
                k2 = (g * TPG + tt) % 2
                layer_norm(B, rt, rb, stt[k2], mv[k2], sd[k2], rs[k2], gB, bB)
                t0 = g * G + tt * 128
                P.dma("sp", out_dst[t0:t0 + 128, :], rt[:], [rb], [], rb)
        allb = xTb + hTb + [w13[1], w2[1], gB[1], bB[1], ident[1], xT[1], hT[1]] + [x[1] for x in r + sl + stt + mv + sd + rs + pT + pg + pu + pd + extra] + w13bufs + w2bufs + wgb
        for en in ("pe", "act", "dve", "pool", "sp"):
            P.wait_all(en, allb)
        P.emit()


def layer_norm(B, rt, rb, st, mv, sd, rs, gB, bB):
    P = B.P
    for h in range(2):
        P.op("dve", lambda e, o=st[0][:, h, :], i=rt[:, h * 512:(h + 1) * 512]: e.bn_stats(out=o, in_=i),
             reads=[rb], writes=[st[1]])
    P.op("dve", lambda e: e.bn_aggr(out=mv[0][:], in_=st[0][:].rearrange("p a b -> p (a b)")),
         reads=[st[1]], writes=[mv[1]])
    P.op("act", lambda e: e.activation(out=sd[0][:], in_=mv[0][:, 1:2], func=AF.Ln, bias=B.eps_ln[0][:], scale=1.0),
         reads=[mv[1], B.eps_ln[1]], writes=[sd[1]])
    P.op("act", lambda e: e.activation(out=rs[0][:], in_=sd[0][:], func=AF.Exp, scale=-0.5), reads=[sd[1]], writes=[rs[1]])
    P.op("dve", lambda e: e.tensor_scalar(out=rt[:], in0=rt[:], scalar1=mv[0][:, 0:1], scalar2=rs[0][:],
                                          op0=ALU.subtract, op1=ALU.mult),
         reads=[rb, mv[1], rs[1]], writes=[rb])
    P.op("pool", lambda e: e.tensor_tensor(out=rt[:], in0=rt[:], in1=gB[0][:], op=ALU.mult),
         reads=[rb, gB[1]], writes=[rb])
    P.op("pool", lambda e: e.tensor_tensor(out=rt[:], in0=rt[:], in1=bB[0][:], op=ALU.add),
         reads=[rb, bB[1]], writes=[rb])


def build_nc(cfg):
    nc = bass.Bass("TRN2", target_bir_lowering=False)
    nseq, seqlen = cfg["nseq"], cfg["seqlen"]
    ntok = nseq * seqlen
    with ExitStack() as stack:
        B = Builder(nc, stack, cfg)
        di = lambda n, shp, dt=F32: B.dram(n, shp, dt, "ExternalInput")
        x_d = di("x", [ntok, D])
        p_d = di("p", [ntok, PLE])
        ident_d = di("ident", [128, 128])
        cst_d = di("cst", [128, 520])
        tri_d = di("tri", [128, 128])
        rot_d = di("rot", [seqlen // 128, 128, 16])
        w13_1 = di("ffn1_w13", [D, 2 * FF])
        w2_1 = di("ffn1_w2", [FF, D])
        w13_2 = di("ffn2_w13", [D, 2 * FF])
        w2_2 = di("ffn2_w2", [FF, D])
        lng = [di(n, [128, D]) for n in ("ln1_g", "ln2_g", "ln3_g")]
        lnb = [di(n, [128, D]) for n in ("ln1_b", "ln2_b", "ln3_b")]
        w_in_d = di("w_in", [D, INW])
        w_out_d = di("w_out", [D, D])
        convw_d = di("convw", [128, 4, 1536])
        gpar_d = di("gpar", [128, 136])
        lpar_d = di("lpar", [128, 256])
        subln_d = di("subln", [128, 1])
        wg_d = di("ple_gate_w", [D, D])
        wp_d = di("ple_proj_w", [PLE, D])
        out_d = B.dram("out", [ntok, D], F32, "ExternalOutput")
        x1_d = B.dram("x1_scr", [ntok, D], F32, "Internal")
        x2_d = B.dram("x2_scr", [ntok, D], F32, "Internal")
        qT_d = B.dram("qT_scr", [512, ntok], BF16, "Internal")
        kT_d = B.dram("kT_scr", [512, ntok], BF16, "Internal")
        v_d = B.dram("v_scr", [ntok, 512], BF16, "Internal")
        yaT_d = B.dram("yaT_scr", [512, ntok], BF16, "Internal")
        ple_d = B.dram("ple_scr", [ntok, D], F32, "Internal")
        B.eps_ln = B.sb("eps_ln", [128, 1], F32)
        B.P.op("pool", lambda e: e.memset(B.eps_ln[0][:], LN_EPS), writes=[B.eps_ln[1]])
        with nc.allow_low_precision("bf16 matmul operands, fp32 accumulation (problem tolerance calibrated for this)"):
            ffn_phase(B, x_d, x1_d, w13_1, w2_1, lng[0], lnb[0], ident_d, ntok, G=512, tag="f1")
            gdn_phase(B, x1_d, w_in_d, cst_d, convw_d, gpar_d, rot_d, qT_d, kT_d, v_d, yaT_d, nseq, seqlen)
            attn_phase(B, x1_d, qT_d, kT_d, v_d, yaT_d, w_out_d, cst_d, tri_d, lpar_d, subln_d, lng[1], lnb[1], x2_d, nseq, seqlen,
                       p_d=p_d, wg_d=wg_d, wp_d=wp_d, ple_d=ple_d)
            ffn_phase(B, x2_d, out_d, w13_2, w2_2, lng[2], lnb[2], ident_d, ntok, G=512, ple=ple_d, tag="f2")
    return nc


def _cst():
    c = np.zeros((128, 520), np.float32)
    idx = np.arange(128)
    same = (idx[:, None] // 64) == (idx[None, :] // 64)
    c[:, 0:128] = np.eye(128)
    c[:, 128:256] = ((idx[:, None] <= idx[None, :]) & same)
    c[:, 256:384] = ((idx[:, None] > idx[None, :]) & same)
    c[:, 384:512] = 1.0
    c[:, 512] = (idx < 64)
    c[:, 513] = (idx >= 64)
    return c


def _rot(seqlen):
    inv = (500000.0 ** (-np.arange(0, 16, 2, dtype=np.float32) / np.float32(16))).astype(np.float32)
    ang = (np.arange(seqlen, dtype=np.float32)[:, None] * inv[None, :]).astype(np.float32)
    r = np.concatenate([np.cos(ang), np.sin(ang)], -1).astype(np.float32)
    return np.ascontiguousarray(r.reshape(seqlen // 128, 128, 16))


def _rep(v, n=128):
    v = np.asarray(v, np.float32)
    return np.ascontiguousarray(np.broadcast_to(v, (n,) + v.shape))


def host_inputs(inputs, seqlen):
    g = lambda k: np.asarray(inputs[k], np.float32)[0]
    idx = np.arange(128)
    com = {
        "ident": np.eye(128, dtype=np.float32), "cst": _cst(),
        "tri": (idx[None, :] >= idx[:, None]).astype(np.float32), "rot": _rot(seqlen),
        "ffn1_w13": g("ffn1_w13"), "ffn1_w2": g("ffn1_w2"), "ffn2_w13": g("ffn2_w13"), "ffn2_w2": g("ffn2_w2"),
        "w_in": g("w_in"), "w_out": g("w_out"), "ple_gate_w": g("ple_gate_w"), "ple_proj_w": g("ple_proj_w"),
        "convw": _rep(g("gdn_conv_w")),
        "gpar": np.ascontiguousarray(np.concatenate([_rep(g("gdn_a_log")), _rep(g("gdn_dt_bias")), _rep(g("gdn_norm_w"))], 1)),
        "lpar": _rep(np.concatenate([g("diff_lq1"), g("diff_lk1"), g("diff_lq2"), g("diff_lk2")])),
        "subln": np.ascontiguousarray(g("diff_subln_w").reshape(128, 1)),
    }
    for n in ("ln1_g", "ln1_b", "ln2_g", "ln2_b", "ln3_g", "ln3_b"):
        com[n] = _rep(g(n))
    return com


def kernel(**inputs):
    x = np.asarray(inputs["x"], np.float32)
    p = np.asarray(inputs["p"], np.float32)[0]
    bsz, seqlen, _ = x.shape
    ncores = 8
    nseq = bsz // ncores
    com = host_inputs(inputs, seqlen)
    nc = build_nc({"nseq": nseq, "seqlen": seqlen})
    in_maps = []
    for c in range(ncores):
        m = dict(com)
        m["x"] = np.ascontiguousarray(x[c * nseq:(c + 1) * nseq].reshape(nseq * seqlen, D))
        m["p"] = np.ascontiguousarray(p[c * nseq:(c + 1) * nseq].reshape(nseq * seqlen, PLE))
        in_maps.append(m)
    res = run_bass_kernel_spmd(nc, in_maps, core_ids=list(range(ncores)))
    out = np.concatenate([np.asarray(r["out"], np.float32).reshape(nseq, seqlen, D) for r in res.results], axis=0)
    return out


C_ID, C_MU, C_MSL, C_ONES, C_CIND = 0, 128, 256, 384, 512
HR_Z, HR_A, HR_B, HR_Q, HR_K, HR_V = 0, 512, 516, 520, 1032, 1544


def r3(ap, h=4):
    return ap.rearrange("p (h d) -> p h d", h=h)


def bc(ap, n, d):
    return ap.unsqueeze(2).to_broadcast([128, n, d])


def bcm(ap, n, d):
    return ap.unsqueeze(1).to_broadcast([128, n, d])


def gdn_phase(B, x1_d, w_in_d, cst_d, convw_d, gpar_d, rot_d, qT_d, kT_d, v_d, yaT_d, nseq, seqlen):
    nc, P = B.nc, B.P
    NT = seqlen // 128
    with ExitStack() as st:
        B.stack = st
        B.banks(st, "g")
        win = B.sb("win", [128, 8, INW], BF16)
        cst = B.sb("cst", [128, 520], F32)
        B.ident = (cst[0][:, 0:128], cst[1])
        convw = B.sb("convw", [128, 4, 1536], F32)
        gpar = B.sb("gpar", [128, 136], F32)
        rot = B.sbn("rot", [128, 16], F32, 2)
        negA = B.sb("negA", [128, 4], F32)
        eps6 = B.sb("eps6", [128, 1], F32)
        one1 = B.sb("one1", [128, 1], F32)
        r = B.sbn("gr", [128, D], F32, 2)
        x1T = B.sbn("x1T", [128, 8, 128], BF16, 2)
        hq = B.sbn("hq", [128, 1536], F32, 2)
        hr = B.sb("hr", [128, 2056], F32)
        hs = B.sbn("hs", [128, 1536], F32, 3)
        hs_halo = [Buf("hsh0"), Buf("hsh1"), Buf("hsh2")]
        cc = B.sb("cc", [128, 1536], F32)
        sm = {n: B.sb("sm_" + n, [128, 8], F32) for n in ("ss", "sd", "rinv", "xa", "ea", "sp", "g", "beta", "be", "so", "sdo", "rio")}
        rt = {n: B.sb("rt_" + n, [128, 16, 8], F32) for n in ("t1", "t2", "t3", "t4")}
        zs = B.sbn("zs", [128, 512], F32, 2)
        vb = B.sbn("vb", [128, 512], BF16, 2)
        qTg = B.sb("qTg", [128, 4, 512], BF16)
        kTg = B.sb("kTg", [128, 4, 512], BF16)
        yTg = B.sb("yTg", [128, 4, 512], BF16)
        gM = B.sb("gM", [128, 4, 384], F32)
        Esm = B.sb("Esm", [128, 16], F32)
        S = B.sb("S", [128, 512], F32)
        T = {n: B.sb("T_" + n, [128, 512], F32) for n in ("decay", "decayT", "qdec", "Af")}
        for n in ("knT", "qnT", "qdecT", "A0", "A1", "B0", "B1", "R0", "R1", "L0", "L1", "qkTm", "bv", "kbg", "kdec", "wT",
                  "vnz0", "vnz1", "Sb"):
            T[n] = B.sb("T_" + n, [128, 512], BF16)
        for a, b_ in (("y", "decay"), ("osb", "decayT"), ("u", "qdec")):
            T[a] = T[b_]

        P.dma("sp", cst[0][:], cst_d, [], [cst[1]], cst[1])
        P.dma("sp", convw[0][:], convw_d, [], [convw[1]], convw[1])
        P.dma("sp", gpar[0][:], gpar_d, [], [gpar[1]], gpar[1])
        winv = w_in_d.rearrange("(c p) f -> p c f", p=128)
        winb = [Buf("win_%d" % c) for c in range(8)]
        for c in range(8):
            P.dma("pool", win[0][:, c, :], winv[:, c, :], [], [winb[c]], winb[c], max_dma_last_dim=8192)
        P.op("pool", lambda e: e.memset(eps6[0][:], RMS_EPS), writes=[eps6[1]])
        P.op("pool", lambda e: e.memset(one1[0][:], 1.0), writes=[one1[1]])
        B.act(negA[0][:], gpar[0][:, 0:4], AF.Exp, [gpar[1]], [negA[1]])
        B.ts("dve", negA[0][:], negA[0][:], -1.0, None, ALU.mult, ALU.bypass, [negA[1]], [negA[1]])
        dtb = gpar[0][:, 4:8]
        normw = gpar[0][:, 8:136]
        MU = cst[0][:, C_MU:C_MU + 128]
        MSL = cst[0][:, C_MSL:C_MSL + 128]
        IDN = cst[0][:, C_ID:C_ID + 128]
        ONEC = cst[0][:, C_ONES:C_ONES + 1]
        CIND = cst[0][:, C_CIND:C_CIND + 2]
        cb = cst[1]
        AX = mybir.AxisListType.X

        def hsl(h):
            return slice(h * 128, (h + 1) * 128)

        qn = cc[0][:, 0:512]
        kn = cc[0][:, 512:1024]
        vv = cc[0][:, 1024:1536]
        gg = sm["g"][0][:, 0:4]
        beta = sm["beta"][0][:, 0:4]
        eG = Esm[0][:, 0:4]
        E2 = Esm[0][:, 4:8]

        def F(s, t):
            tok0 = s * seqlen + t * 128
            g4 = t % 4
            rtile, rb = r[t % 2]
            xT, xTb = x1T[t % 2]
            hqc, hqcb = hq[t % 2]
            hqp, hqpb = hq[(t + 1) % 2]
            rott, rotb = rot[t % 2]
            zst, zsb = zs[t % 2]
            P.dma("sp", rtile[:], x1_d[tok0:tok0 + 128, :], [], [rb], rb)
            P.dma("sp", rott[:], rot_d[t], [], [rotb], rotb)
            for half in range(2):
                pt, pb = B.bank()
                for cc_ in range(4):
                    c = half * 4 + cc_
                    B.tr(pt[:, cc_ * 128:(cc_ + 1) * 128], pb, rtile[:, c * 128:(c + 1) * 128], [rb], inc=(cc_ == 3))
                B.cp("act" if half == 0 else "dve", xT[:, half * 4:half * 4 + 4, :], r3(pt[:]), [pb], [xTb])
                yield
            for gi in range(8):
                c0 = gi * 512
                n = min(512, INW - c0)
                pt, pb = B.bank()
                for c in range(8):
                    B.mm(pt[:, 0:n], pb, xT[:, c, :], win[0][:, c, c0:c0 + n], [xTb, winb[c]], start=(c == 0), stop=(c == 7))
                if gi < 3:
                    B.cp("act" if gi % 2 == 0 else "dve", hqc[:, c0:c0 + n], pt[:, 0:n], [pb], [hqcb])
                else:
                    B.cp("act" if gi % 2 == 0 else "dve", hr[0][:, c0 - 1536:c0 - 1536 + n], pt[:, 0:n], [pb], [hr[1]])
                yield
            B.tt("pool", cc[0][:], hqc[:], convw[0][:, 3, :], ALU.mult, [hqcb, convw[1]], [cc[1]])
            for sft in (1, 2, 3):
                k = sft - 1
                hst, hsb = hs[k]
                P.dma("sp", hst[sft:128, :], hqc[0:128 - sft, :], [hqcb], [hsb], hsb)
                if t == 0:
                    P.op("pool", lambda e, o=hst[0:sft, :]: e.memset(o, 0.0), writes=[hs_halo[k]])
                else:
                    P.dma("sp", hst[0:sft, :], hqp[128 - sft:128, :], [hqpb], [hs_halo[k]], hs_halo[k])
            yield
            for sft in (3, 2, 1):
                k = sft - 1
                hst, hsb = hs[k]
                B.tt("pool", hst[:], hst[:], convw[0][:, 3 - sft, :], ALU.mult, [hsb, hs_halo[k], convw[1]], [hsb, hs_halo[k]])
                B.tt("dve", cc[0][:], cc[0][:], hst[:], ALU.add, [cc[1], hsb, hs_halo[k]], [cc[1]])
                yield
            scr, scrb = hs[1]
            B.act(scr[:], cc[0][:], AF.Exp, [cc[1]], [scrb, hs_halo[1]], scale=-1.0)
            B.act(scr[:], scr[:], AF.Ln, [scrb, one1[1]], [scrb], bias=one1[0][:], scale=1.0)
            B.act(scr[:], scr[:], AF.Exp, [scrb], [scrb], scale=-1.0)
            B.tt("dve", cc[0][:], cc[0][:], scr[:], ALU.mult, [cc[1], scrb], [cc[1]])
            B.tt("dve", sm["xa"][0][:, 0:4], hr[0][:, HR_A:HR_A + 4], dtb, ALU.add, [hr[1], gpar[1]], [sm["xa"][1]])
            B.act(sm["ea"][0][:, 0:4], sm["xa"][0][:, 0:4], AF.Exp, [sm["xa"][1]], [sm["ea"][1]])
            B.act(sm["sp"][0][:, 0:4], sm["ea"][0][:, 0:4], AF.Ln, [sm["ea"][1], one1[1]], [sm["sp"][1]], bias=one1[0][:], scale=1.0)
            B.tt("dve", gg, sm["sp"][0][:, 0:4], negA[0][:], ALU.mult, [sm["sp"][1], negA[1]], [sm["g"][1]])
            B.act(beta, hr[0][:, HR_B:HR_B + 4], AF.Exp, [hr[1]], [sm["beta"][1]], scale=-1.0)
            B.act(beta, beta, AF.Ln, [sm["beta"][1], one1[1]], [sm["beta"][1]], bias=one1[0][:], scale=1.0)
            B.act(beta, beta, AF.Exp, [sm["beta"][1]], [sm["beta"][1]], scale=-1.0)
            B.act(zst[:], hr[0][:, HR_Z:HR_Z + 512], AF.Exp, [hr[1]], [zsb], scale=-1.0)
            B.act(zst[:], zst[:], AF.Ln, [zsb, one1[1]], [zsb], bias=one1[0][:], scale=1.0)
            B.act(zst[:], zst[:], AF.Exp, [zsb], [zsb], scale=-1.0)
            B.tt("pool", zst[:], zst[:], hr[0][:, HR_Z:HR_Z + 512], ALU.mult, [zsb, hr[1]], [zsb])
            yield
            sq = (hs[0][0][:, 0:1024], hs[0][1])
            B.tt("pool", sq[0], cc[0][:, 0:1024], cc[0][:, 0:1024], ALU.mult, [cc[1]], [sq[1], hs_halo[0]])
            P.op("dve", lambda e: e.tensor_reduce(out=sm["ss"][0][:], in_=r3(sq[0], 8), axis=AX, op=ALU.add),
                 reads=[sq[1]], writes=[sm["ss"][1]])
            B.act(sm["sd"][0][:], sm["ss"][0][:], AF.Ln, [sm["ss"][1], eps6[1]], [sm["sd"][1]], bias=eps6[0][:], scale=1.0)
            B.act(sm["rinv"][0][:], sm["sd"][0][:], AF.Exp, [sm["sd"][1]], [sm["rinv"][1]], scale=-0.5)
            B.ts("dve", sm["rinv"][0][:, 0:4], sm["rinv"][0][:, 0:4], 128.0 ** -0.5, None, ALU.mult, ALU.bypass,
                 [sm["rinv"][1]], [sm["rinv"][1]])
            B.tt("dve", r3(cc[0][:, 0:1024], 8), r3(cc[0][:, 0:1024], 8), bc(sm["rinv"][0][:], 8, 128), ALU.mult,
                 [cc[1], sm["rinv"][1]], [cc[1]])
            yield
            qk3 = hr[0][:, HR_Q:HR_Q + 1024].rearrange("p (m d) -> p m d", m=16)
            x1v, x2v = qk3[:, :, 0:8], qk3[:, :, 8:16]
            cosb, sinb = bcm(rott[:, 0:8], 16, 8), bcm(rott[:, 8:16], 16, 8)
            rtb = [rt[n][1] for n in ("t1", "t2", "t3", "t4")]
            B.tt("pool", rt["t1"][0][:], x1v, cosb, ALU.mult, [hr[1], rotb], [rtb[0]])
            B.tt("pool", rt["t2"][0][:], x2v, sinb, ALU.mult, [hr[1], rotb], [rtb[1]])
            B.tt("pool", rt["t3"][0][:], x2v, cosb, ALU.mult, [hr[1], rotb], [rtb[2]])
            B.tt("pool", rt["t4"][0][:], x1v, sinb, ALU.mult, [hr[1], rotb], [rtb[3]])
            B.tt("pool", x1v, rt["t1"][0][:], rt["t2"][0][:], ALU.subtract, [rtb[0], rtb[1], hr[1]], [hr[1]])
            B.tt("pool", x2v, rt["t3"][0][:], rt["t4"][0][:], ALU.add, [rtb[2], rtb[3], hr[1]], [hr[1]])
            yield
            tc0 = g4 * 128
            pt, pb = B.bank()
            for hc in range(4):
                B.tr(pt[:, hsl(hc)], pb, hr[0][:, HR_Q + hc * 128:HR_Q + (hc + 1) * 128], [hr[1]], inc=(hc == 3))
            B.act(qTg[0][:, :, tc0:tc0 + 128], r3(pt[:]), AF.Copy, [pb], [qTg[1]], scale=0.125)
            pt, pb = B.bank()
            for hc in range(4):
                B.tr(pt[:, hsl(hc)], pb, hr[0][:, HR_K + hc * 128:HR_K + (hc + 1) * 128], [hr[1]], inc=(hc == 3))
            B.cp("dve", kTg[0][:, :, tc0:tc0 + 128], r3(pt[:]), [pb], [kTg[1]])
            vbt, vbb = vb[t % 2]
            B.cp("act", vbt[:], hr[0][:, HR_V:HR_V + 512], [hr[1]], [vbb])
            P.dma("sp", v_d[tok0:tok0 + 128, :], vbt[:], [vbb], [], vbb)
            if g4 == 3:
                gt0 = tok0 - 384
                for (src, dst) in ((qTg, qT_d), (kTg, kT_d)):
                    P.dma("sp", dst.rearrange("(h p) t -> p h t", p=128)[:, :, gt0:gt0 + 512], src[0][:], [src[1]], [], src[1])
            yield

        def M(s, t):
            B.tt("dve", gM[0][:], bcm(cst[0][:, C_MU:C_MU + 384], 4, 384), bc(gg, 4, 384), ALU.mult, [cb, sm["g"][1]], [gM[1]])
            pE0, pE0b = B.bank()
            pE1, pE1b = B.bank()
            pE2, pE2b = B.bank()
            for h in range(4):
                gU = gM[0][:, h, 0:128]
                gSL = gM[0][:, h, 128:256]
                gON = gM[0][:, h, 256:384]
                B.mm(pE0[:, hsl(h)], pE0b, gU, MSL, [gM[1], cb])
                B.mm(pE1[:, hsl(h)], pE1b, MSL, gU, [gM[1], cb])
                B.mm(pE2[:, h:h + 1], pE2b, gU, ONEC, [gM[1], cb])
                B.mm(pE2[:, 4 + h:5 + h], pE2b, gSL, ONEC, [gM[1], cb])
                B.mm(pE2[:, 8 + h:9 + h], pE2b, gON, CIND[:, 0:1], [gM[1], cb])
                B.mm(pE2[:, 12 + h:13 + h], pE2b, gON, CIND[:, 1:2], [gM[1], cb])
            dec, decb = T["decay"]
            decT, decTb = T["decayT"]
            B.act(Esm[0][:], pE2[:, 0:16], AF.Exp, [pE2b], [Esm[1]])
            B.act(dec[:], pE0[:], AF.Exp, [pE0b], [decb])
            B.act(decT[:], pE1[:], AF.Exp, [pE1b], [decTb])
            qd, qdb = T["qdec"]
            B.tt("dve", r3(qd[:]), r3(qn), bc(eG, 4, 128), ALU.mult, [cc[1], Esm[1]], [qdb])
            for (src, srcb, dstn, eng) in ((kn, cc[1], "knT", "act"), (qn, cc[1], "qnT", "dve"), (qd[:], qdb, "qdecT", "act")):
                pt, pb = B.bank()
                for h in range(4):
                    B.tr(pt[:, hsl(h)], pb, src[:, hsl(h)], [srcb], inc=(h == 3))
                B.cp(eng, T[dstn][0][:], pt[:], [pb], [T[dstn][1]])
            B.tt("pool", r3(dec[:]), r3(dec[:]), bcm(MSL, 4, 128), ALU.mult, [decb, cb], [decb])
            B.tt("pool", r3(dec[:]), r3(dec[:]), bc(beta, 4, 128), ALU.mult, [decb, sm["beta"][1]], [decb])
            B.tt("pool", r3(decT[:]), r3(decT[:]), bcm(MU, 4, 128), ALU.mult, [decTb, cb], [decTb])
            knT, knTb = T["knT"]
            qnT, qnTb = T["qnT"]
            B.tt("dve", sm["be"][0][:, 0:4], beta, eG, ALU.mult, [sm["beta"][1], Esm[1]], [sm["be"][1]])
            bv, bvb = T["bv"]
            kbg, kbgb = T["kbg"]
            kdec, kdecb = T["kdec"]
            B.tt("pool", r3(bv[:]), r3(vv), bc(beta, 4, 128), ALU.mult, [cc[1], sm["beta"][1]], [bvb])
            B.tt("pool", r3(kbg[:]), r3(kn), bc(sm["be"][0][:, 0:4], 4, 128), ALU.mult, [cc[1], sm["be"][1]], [kbgb])
            B.tt("pool", r3(kdec[:]), r3(kn), bc(E2, 4, 128), ALU.mult, [cc[1], Esm[1]], [kdecb])
            pt, pb = B.bank()
            for h in range(4):
                B.mm(pt[:, hsl(h)], pb, knT[:, hsl(h)], knT[:, hsl(h)], [knTb])
            Af, Afb = T["Af"]
            A, Ab = T["A0"]
            B.tt("dve", Af[:], pt[:], dec[:], ALU.mult, [pb, decb], [Afb])
            pt2, pb2 = B.bank()
            for h in range(4):
                B.mm(pt2[:, hsl(h)], pb2, knT[:, hsl(h)], qnT[:, hsl(h)], [knTb, qnTb])
            pt, pb = B.bank()
            for h in range(4):
                B.tr(pt[:, hsl(h)], pb, Af[:, hsl(h)], [Afb], inc=(h == 3))
            B.cp("pool", A[:], Af[:], [Afb], [Ab])
            Bm, Bb = T["B0"]
            B.cp("act", Bm[:], pt[:], [pb], [Bb])
            qkTm, qkTmb = T["qkTm"]
            B.tt("dve", qkTm[:], pt2[:], decT[:], ALU.mult, [pb2, decTb], [qkTmb])
            R, Rb = T["R0"]
            L, Lb = T["L0"]
            B.stt(r3(L[:]), r3(Af[:]), -1.0, bcm(IDN, 4, 128), ALU.mult, ALU.add, [Afb, cb], [Lb])
            B.stt(r3(R[:]), r3(pt[:]), -1.0, bcm(IDN, 4, 128), ALU.mult, ALU.add, [pb, cb], [Rb])

        def C(s, t):
            tok0 = s * seqlen + t * 128
            g4 = t % 4
            tc0 = g4 * 128
            zst, zsb = zs[t % 2]
            A, Ab = T["A0"]
            Bm, Bb = T["B0"]
            R, Rb = T["R0"]
            L, Lb = T["L0"]
            qdT, qdTb = T["qdecT"]
            qkTm, qkTmb = T["qkTm"]
            bv, bvb = T["bv"]
            kbg, kbgb = T["kbg"]
            kdec, kdecb = T["kdec"]
            def squares(k, A, Ab, Bm, Bb):
                An, Anb = T["A%d" % (k % 2)]
                Bn, Bnb = T["B%d" % (k % 2)]
                ptB, pbB = B.bank()
                for h in range(4):
                    B.mm(ptB[:, hsl(h)], pbB, A[:, hsl(h)], Bm[:, hsl(h)], [Ab, Bb])
                ptA = pbA = None
                if k < 5:
                    ptA, pbA = B.bank()
                    for h in range(4):
                        B.mm(ptA[:, hsl(h)], pbA, Bm[:, hsl(h)], A[:, hsl(h)], [Ab, Bb])
                return (An, Anb, Bn, Bnb, ptA, pbA, ptB, pbB)

            def evac_squares(k, sqr):
                An, Anb, Bn, Bnb, ptA, pbA, ptB, pbB = sqr
                B.cp("act", Bn[:], ptB[:], [pbB], [Bnb])
                if k < 5:
                    B.cp("dve", An[:], ptA[:], [pbA], [Anb])

            sqr = squares(1, A, Ab, Bm, Bb)
            evac_squares(1, sqr)
            yield
            for k in range(1, 6):
                An, Anb, Bn, Bnb = sqr[0], sqr[1], sqr[2], sqr[3]
                Rn, Rnb = T["R%d" % (k % 2)]
                Ln, Lnb = T["L%d" % (k % 2)]
                ptR, pbR = B.bank()
                for h in range(4):
                    B.mm(ptR[:, hsl(h)], pbR, L[:, hsl(h)], Bn[:, hsl(h)], [Lb, Bnb])
                if k < 5:
                    ptL, pbL = B.bank()
                    for h in range(4):
                        B.mm(ptL[:, hsl(h)], pbL, R[:, hsl(h)], An[:, hsl(h)], [Rb, Anb])
                    nsqr = squares(k + 1, An, Anb, Bn, Bnb)
                B.tt("dve", Rn[:], ptR[:], R[:], ALU.add, [pbR, Rb], [Rnb])
                if k < 5:
                    B.tt("dve", Ln[:], ptL[:], L[:], ALU.add, [pbL, Lb], [Lnb])
                    evac_squares(k + 1, nsqr)
                    sqr = nsqr
                A, Ab, Bm, Bb, R, Rb, L, Lb = An, Anb, Bn, Bnb, Rn, Rnb, Ln, Lnb
                yield
            pt, pb = B.bank()
            for h in range(4):
                B.mm(pt[:, hsl(h)], pb, R[:, hsl(h)], bv[:, hsl(h)], [Rb, bvb])
            pt2, pb2 = B.bank()
            for h in range(4):
                B.mm(pt2[:, hsl(h)], pb2, kbg[:, hsl(h)], R[:, hsl(h)], [Rb, kbgb])
            u, ub = T["u"]
            B.cp("act", u[:], pt[:], [pb], [ub])
            wT, wTb = T["wT"]
            B.cp("dve", wT[:], pt2[:], [pb2], [wTb])
            yield
            osb, osbb = T["osb"]
            Sb, Sbb = T["Sb"]
            for c in range(2):
                rows = slice(64 * c, 64 * c + 64)
                vn, vnb = T["vnz%d" % c]
                pw, pwb = B.bank()
                for h in range(4):
                    B.mm(pw[:, hsl(h)], pwb, wT[:, hsl(h)], Sb[:, hsl(h)], [wTb, Sbb])
                B.tt("dve", vn[rows, :], u[rows, :], pw[rows, :], ALU.subtract, [ub, pwb], [vnb])
                yield
                po, pob = B.bank()
                for h in range(4):
                    B.mm(po[:, hsl(h)], pob, qdT[:, hsl(h)], Sb[:, hsl(h)], [qdTb, Sbb], start=True, stop=False)
                    B.mm(po[:, hsl(h)], pob, qkTm[:, hsl(h)], vn[:, hsl(h)], [qkTmb, vnb], start=False, stop=True)
                pS, pSb = B.bank()
                for h in range(4):
                    B.mm(pS[:, hsl(h)], pSb, kdec[:, hsl(h)], vn[:, hsl(h)], [kdecb, vnb])
                B.cp("act", osb[rows, :], po[rows, :], [pob], [osbb])
                eGl = Esm[0][:, 8 + 4 * c:12 + 4 * c]
                B.tt("dve", r3(S[0][:]), r3(S[0][:]), bc(eGl, 4, 128), ALU.mult, [S[1], Esm[1]], [S[1]])
                B.tt("dve", S[0][:], S[0][:], pS[:], ALU.add, [S[1], pSb], [S[1]])
                B.cp("act", Sb[:], S[0][:], [S[1]], [Sbb])
                yield
            y, yb = T["y"]
            B.tt("pool", y[:], osb[:], osb[:], ALU.mult, [osbb], [yb])
            P.op("dve", lambda e: e.tensor_reduce(out=sm["so"][0][:, 0:4], in_=r3(y[:]), axis=AX, op=ALU.add),
                 reads=[yb], writes=[sm["so"][1]])
            B.act(sm["sdo"][0][:, 0:4], sm["so"][0][:, 0:4], AF.Ln, [sm["so"][1], eps6[1]], [sm["sdo"][1]], bias=eps6[0][:], scale=1.0 / 128.0)
            B.act(sm["rio"][0][:, 0:4], sm["sdo"][0][:, 0:4], AF.Exp, [sm["sdo"][1]], [sm["rio"][1]], scale=-0.5)
            yield
            B.tt("dve", r3(y[:]), r3(osb[:]), bc(sm["rio"][0][:, 0:4], 4, 128), ALU.mult, [osbb, sm["rio"][1], yb], [yb])
            B.tt("pool", r3(y[:]), r3(y[:]), bcm(normw, 4, 128), ALU.mult, [yb, gpar[1]], [yb])
            B.tt("pool", y[:], y[:], zst[:], ALU.mult, [yb, zsb], [yb])
            yield
            pt, pb = B.bank()
            for h in range(4):
                B.tr(pt[:, hsl(h)], pb, y[:, hsl(h)], [yb], inc=(h == 3))
            B.cp("act", yTg[0][:, :, tc0:tc0 + 128], r3(pt[:]), [pb], [yTg[1]])
            if g4 == 3:
                gt0 = tok0 - 384
                P.dma("sp", yaT_d.rearrange("(h p) t -> p h t", p=128)[:, :, gt0:gt0 + 512], yTg[0][:], [yTg[1]], [], yTg[1])
            yield

        for s in range(nseq):
            P.op("pool", lambda e: e.memset(S[0][:], 0.0), writes=[S[1]])
            P.op("pool", lambda e: e.memset(T["Sb"][0][:], 0.0), writes=[T["Sb"][1]])
            if s == 0:
                for c in range(2):
                    P.op("pool", lambda e, o=T["vnz%d" % c][0][:]: e.memset(o, 0.0), writes=[T["vnz%d" % c][1]])
            for _ in F(s, 0):
                pass
            for t in range(NT):
                M(s, t)
                gens = [C(s, t)]
                if t + 1 < NT:
                    gens.append(F(s, t + 1))
                while gens:
                    for gen in list(gens):
                        try:
                            next(gen)
                        except StopIteration:
                            gens.remove(gen)
        allb = [x[1] for x in [win, cst, convw, gpar, negA, eps6, one1, hr, cc, qTg, kTg, yTg, gM, Esm, S]]
        allb += [x[1] for x in rot + r + x1T + hq + hs + vb + zs] + hs_halo + winb
        allb += [x[1] for x in sm.values()] + [x[1] for x in rt.values()] + [x[1] for x in T.values()] + [x[1] for x in B.pbanks]
        for en in ("pe", "act", "dve", "pool", "sp"):
            P.wait_all(en, allb)
        P.emit()


def attn_phase(B, x1_d, qT_d, kT_d, v_d, yaT_d, wout_d, cst_d, tri_d, lpar_d, subln_d, lng_d, lnb_d, x2_d, nseq, seqlen,
               p_d=None, wg_d=None, wp_d=None, ple_d=None):
    nc, P = B.nc, B.P
    NT = seqlen // 128
    NG = seqlen // 512
    NPC = max(1, NT // 8)
    with ExitStack() as st:
        B.stack = st
        B.banks(st, "a")
        B.rotbanks = [4, 5, 6, 7]
        pOs = [B.pbanks[0], B.pbanks[1]]
        pLs = [B.pbanks[2], B.pbanks[3]]
        KT = B.sb("KT", [128, 4, seqlen], BF16)
        V = B.sb("V", [128, NT, 512], BF16)
        KTb = [Buf("KT%d" % i) for i in range(NPC)]
        Vb = [Buf("V%d" % i) for i in range(NPC)]
        wout = B.sb("wout", [128, 8, D], BF16)
        cst = B.sb("acst", [128, 520], F32)
        B.ident = (cst[0][:, 0:128], cst[1])
        ONES = cst[0][:, C_ONES:C_ONES + 128]
        tri = B.sb("tri", [128, 128], F32)
        lpar = B.sb("lpar", [128, 256], F32)
        subln = B.sb("subln", [128, 1], F32)
        gB = B.sb("agB", [128, D], F32)
        bB = B.sb("abB", [128, D], F32)
        lam = {n: B.sb("lam_" + n, [128, 64], F32) for n in ("p1", "p2")}
        lsm = {n: B.sb("lsm_" + n, [128, 1], F32) for n in ("s1", "s2", "e1", "e2", "nl")}
        qz = [B.sbn("aqz%d_" % m, [128, 4, 512], BF16, 2) for m in range(2)]
        yaTg = B.sbn("ayaTg", [128, 4, 512], BF16, 2)
        ybT = B.sb("ybT", [128, 4, 512], BF16)
        ybTb = [Buf("ybT%d" % h) for h in range(4)]
        NR = 6
        r = B.sbn("ar", [128, D], F32, NR)
        PT = B.sbn("PT", [128, 512], BF16, 3)
        onesb = B.sb("onesb", [128, 128], BF16)
        rl = B.sb("rl", [128, 512], F32)
        Om = B.sb("Om", [128, 512], F32)
        att = B.sbn("att", [128, 512], F32, 2)
        sqa = B.sb("sqa", [128, 512], F32)
        sdn = B.sb("sdn", [128, 512], F32)
        eps6 = B.sb("aeps6", [128, 1], F32)
        stt_ = B.sbn("ast", [128, 2, 6], F32, 2)
        mv = B.sbn("amv", [128, 2], F32, 2)
        sd = B.sbn("asd", [128, 1], F32, 2)
        rs = B.sbn("ars", [128, 1], F32, 2)

        extra = []
        if ple_d is not None:
            wg = B.sb("awg", [128, 8, D], BF16)
            wp = B.sb("awp", [128, 2, D], BF16)
            x2T = B.sbn("ax2T", [128, 8, 128], BF16, 2)
            plebuf = B.sbn("aple", [128, D], F32, 2)
            ptile = B.sbn("aptile", [128, PLE], F32, 2)
            pTt = B.sbn("apTt", [128, 2, 128], BF16, 2)
            sg = B.sb("asg", [128, 512], F32)
            extra = [wg, wp, sg] + x2T + plebuf + ptile + pTt
            wgv = wg_d.rearrange("(c p) f -> p c f", p=128)
            wpv = wp_d.rearrange("(c p) f -> p c f", p=128)
            wgp = [Buf("awg_%d" % c) for c in range(3)]
            wgb = [wgp[0], wgp[0], wgp[1], wgp[1], wgp[2]]
            for c in range(2):
                P.dma("pool", wg[0][:, 4 * c:4 * c + 4, :], wgv[:, 4 * c:4 * c + 4, :], [], [wgp[c]], wgp[c], max_dma_last_dim=8192)
            P.dma("pool", wp[0][:], wpv, [], [wgp[2]], wgp[2], max_dma_last_dim=8192)
            extra_b = wgp
        P.dma("sp", cst[0][:], cst_d, [], [cst[1]], cst[1])
        P.dma("sp", tri[0][:], tri_d, [], [tri[1]], tri[1])
        P.dma("sp", lpar[0][:], lpar_d, [], [lpar[1]], lpar[1])
        P.dma("sp", subln[0][:], subln_d, [], [subln[1]], subln[1])
        P.dma("sp", gB[0][:], lng_d, [], [gB[1]], gB[1])
        P.dma("sp", bB[0][:], lnb_d, [], [bB[1]], bB[1])
        woutv = wout_d.rearrange("(c p) f -> p c f", p=128)
        woutp = [Buf("wout_%d" % c) for c in range(2)]
        woutb = [woutp[c // 2] for c in range(4)]
        for c in range(2):
            P.dma("pool", wout[0][:, 4 * c:4 * c + 4, :], woutv[:, 4 * c:4 * c + 4, :], [], [woutp[c]], woutp[c], max_dma_last_dim=8192)
        P.op("pool", lambda e: e.memset(eps6[0][:], RMS_EPS), writes=[eps6[1]])
        P.op("pool", lambda e: e.memset(onesb[0][:], 1.0), writes=[onesb[1]])
        for m in range(2):
            for i in range(2):
                P.op("pool", lambda e, o=qz[m][i][0][64 * (1 - m):64 * (1 - m) + 64, :, :]: e.memset(o, 0.0), writes=[qz[m][i][1]])
        for i, (pn, sn, en) in enumerate((("p1", "s1", "e1"), ("p2", "s2", "e2"))):
            B.tt("dve", lam[pn][0][:], lpar[0][:, 128 * i:128 * i + 64], lpar[0][:, 128 * i + 64:128 * i + 128], ALU.mult, [lpar[1]], [lam[pn][1]])
            P.op("dve", lambda e, o=lsm[sn][0][:], i_=lam[pn][0][:]: e.tensor_reduce(out=o, in_=i_, axis=mybir.AxisListType.X, op=ALU.add),
                 reads=[lam[pn][1]], writes=[lsm[sn][1]])
            B.act(lsm[en][0][:], lsm[sn][0][:], AF.Exp, [lsm[sn][1]], [lsm[en][1]])
        B.tt("dve", lsm["nl"][0][:], lsm["e2"][0][:], lsm["e1"][0][:], ALU.subtract, [lsm["e1"][1], lsm["e2"][1]], [lsm["nl"][1]])
        B.ts("dve", lsm["nl"][0][:], lsm["nl"][0][:], -LAM_INIT, None, ALU.add, ALU.bypass, [lsm["nl"][1]], [lsm["nl"][1]])
        neglam = lsm["nl"]

        kTv = kT_d.rearrange("(h p) t -> p h t", p=128)
        qTv = qT_d.rearrange("(h p) t -> p h t", p=128)
        yaTv = yaT_d.rearrange("(h p) t -> p h t", p=128)
        ti = 0
        for s in range(nseq):
            s0 = s * seqlen
            for i in range(NPC):
                n = seqlen // NPC
                P.dma("sp", KT[0][:, :, i * n:(i + 1) * n], kTv[:, :, s0 + i * n:s0 + (i + 1) * n], [], [KTb[i]], KTb[i])
                nt = NT // NPC
                P.dma("sp", V[0][:, i * nt:(i + 1) * nt, :],
                      v_d[s0 + i * n:s0 + (i + 1) * n, :].rearrange("(t p) f -> p t f", p=128), [], [Vb[i]], Vb[i])
            for g in range(NG):
                g0 = s0 + g * 512
                yg, ygb = yaTg[g % 2]
                qzm = []
                for m in range(2):
                    qzt, qzb = qz[m][g % 2]
                    P.dma("sp", qzt[64 * m:64 * m + 64, :, :], qTv[64 * m:64 * m + 64, :, g0:g0 + 512], [], [qzb], qzb)
                    qzm.append((qzt, qzb))
                P.dma("sp", yg[:], yaTv[:, :, g0:g0 + 512], [], [ygb], ygb)
                slots = []
                for tt in range(4):
                    rt_, rb = r[ti % NR]
                    ti += 1
                    slots.append((rt_, rb))
                    P.dma("sp", rt_[:], x1_d[g0 + tt * 128:g0 + (tt + 1) * 128, :], [], [rb], rb)
                nkb = 4 * (g + 1)
                pending = []
                for h in range(4):
                    at, atb = att[h % 2]
                    for m in range(2):
                        rows = slice(64 * m, 64 * m + 64)
                        pO, pOb = pOs[m]
                        pL, pLb = pLs[m]
                        LA = 2
                        pSq = {}

                        def issue_scores(kb_):
                            j_ = kb_ - 4 * g
                            q0_ = 128 * j_ if j_ > 0 else 0
                            pS_, pSb_ = B.bank()
                            pc_ = min(kb_ // 8, NPC - 1) if NPC > 1 else 0
                            B.mm(pS_[:, q0_:512], pSb_, KT[0][:, h, kb_ * 128:(kb_ + 1) * 128], qzm[m][0][:, h, q0_:512], [KTb[pc_], qzm[m][1]])
                            pSq[kb_] = (pS_, pSb_)

                        for kb_ in range(min(LA, nkb)):
                            issue_scores(kb_)
                        for kb in range(nkb):
                            j = kb - 4 * g
                            q0 = 128 * j if j > 0 else 0
                            if kb + LA < nkb:
                                issue_scores(kb + LA)
                            pS, pSb = pSq.pop(kb)
                            pc = min(kb // 8, NPC - 1) if NPC > 1 else 0
                            pt_, ptb = PT[kb % 3]
                            B.act(pt_[:, q0:512], pS[:, q0:512], AF.Exp, [pSb], [ptb])
                            if j >= 0:
                                B.tt("pool", pt_[:, q0:q0 + 128], pt_[:, q0:q0 + 128], tri[0][:], ALU.mult, [ptb, tri[1]], [ptb])
                            B.mm(pO[:, q0:512], pOb, V[0][:, kb, h * 128:(h + 1) * 128], pt_[:, q0:512], [Vb[pc], ptb],
                                 start=(kb == 0), stop=(kb == nkb - 1))
                            B.mm(pL[:, q0:512], pLb, onesb[0][:], pt_[:, q0:512], [onesb[1], ptb],
                                 start=(kb == 0), stop=(kb == nkb - 1))
                        B.act(rl[0][:], pL[:], AF.Ln, [pLb], [rl[1]])
                        B.act(rl[0][:], rl[0][:], AF.Exp, [rl[1]], [rl[1]], scale=-1.0)
                        if m == 0:
                            while pending:
                                pending.pop(0)()
                            B.tt("dve", at[:], pO[:], rl[0][:], ALU.mult, [pOb, rl[1]], [atb])
                        else:
                            B.tt("dve", Om[0][:], pO[:], rl[0][:], ALU.mult, [pOb, rl[1]], [Om[1]])
                            B.stt(at[:], Om[0][:], neglam[0][:], at[:], ALU.mult, ALU.add, [Om[1], neglam[1], atb], [atb])
                    def head_epilogue(h=h, at=at, atb=atb):
                        B.tt("pool", sqa[0][:], at[:], at[:], ALU.mult, [atb], [sqa[1]])
                        pN, pNb = B.bank()
                        B.mm(pN[:], pNb, ONES, sqa[0][:], [cst[1], sqa[1]])
                        B.act(sdn[0][:], pN[:], AF.Ln, [pNb, eps6[1]], [sdn[1]], bias=eps6[0][:], scale=1.0 / 128.0)
                        B.act(sdn[0][:], sdn[0][:], AF.Exp, [sdn[1]], [sdn[1]], scale=-0.5)
                        B.tt("dve", at[:], at[:], sdn[0][:], ALU.mult, [atb, sdn[1]], [atb])
                        B.ts("dve", ybT[0][:, h, :], at[:], subln[0][:], 1.0 - LAM_INIT, ALU.mult, ALU.mult, [atb, subln[1]], [ybTb[h]])
                    pending.append(head_epilogue)
                while pending:
                    pending.pop(0)()
                for tt in range(4):
                    rt_, rb = slots[tt]
                    for dh in range(2):
                        pd, pdb = B.bank()
                        for c in range(8):
                            if c < 4:
                                lh, lb = yg[:, c, tt * 128:(tt + 1) * 128], ygb
                            else:
                                lh, lb = ybT[0][:, c - 4, tt * 128:(tt + 1) * 128], ybTb[c - 4]
                            B.mm(pd[:], pdb, lh, wout[0][:, c, dh * 512:(dh + 1) * 512], [lb, woutb[c // 2]], start=(c == 0), stop=(c == 7))
                        B.stt(rt_[:, dh * 512:(dh + 1) * 512], rt_[:, dh * 512:(dh + 1) * 512], ALPHA, pd[:], ALU.mult, ALU.add, [pdb, rb], [rb])
                    k2 = (g * 4 + tt) % 2
                    layer_norm(B, rt_, rb, stt_[k2], mv[k2], sd[k2], rs[k2], gB, bB)
                    t0 = g0 + tt * 128
                    P.dma("sp", x2_d[t0:t0 + 128, :], rt_[:], [rb], [], rb)
                if ple_d is not None:
                    for tt in range(4):
                        rt_, rb = slots[tt]
                        t0 = g0 + tt * 128
                        k2 = (g * 4 + tt) % 2
                        xT2, xT2b = x2T[k2]
                        for half in range(2):
                            pt, pb = B.bank()
                            for cc_ in range(4):
                                c = half * 4 + cc_
                                B.tr(pt[:, cc_ * 128:(cc_ + 1) * 128], pb, rt_[:, c * 128:(c + 1) * 128], [rb], inc=(cc_ == 3))
                            B.cp("act" if half == 0 else "dve", xT2[:, half * 4:half * 4 + 4, :], r3(pt[:]), [pb], [xT2b])
                        ptl, ptlb = ptile[k2]
                        pTx, pTxb = pTt[k2]
                        P.dma("sp", ptl[:], p_d[t0:t0 + 128, :], [], [ptlb], ptlb)
                        pt, pb = B.bank()
                        for c in range(2):
                            B.tr(pt[:, c * 128:(c + 1) * 128], pb, ptl[:, c * 128:(c + 1) * 128], [ptlb], inc=(c == 1))
                        B.cp("act", pTx[:], r3(pt[:, 0:256], 2), [pb], [pTxb])
                        plt, plb = plebuf[k2]
                        for dh in range(2):
                            pgt, pgb = B.bank()
                            for c in range(8):
                                B.mm(pgt[:], pgb, xT2[:, c, :], wg[0][:, c, dh * 512:(dh + 1) * 512], [xT2b, wgb[c // 2]], start=(c == 0), stop=(c == 7))
                            ppt, ppb = B.bank()
                            for c in range(2):
                                B.mm(ppt[:], ppb, pTx[:, c, :], wp[0][:, c, dh * 512:(dh + 1) * 512], [pTxb, wgb[4]], start=(c == 0), stop=(c == 1))
                            B.act(sg[0][:], pgt[:], AF.Sigmoid, [pgb], [sg[1]])
                            B.tt("dve", plt[:, dh * 512:(dh + 1) * 512], sg[0][:], ppt[:], ALU.mult, [sg[1], ppb], [plb])
                        P.dma("sp", ple_d[t0:t0 + 128, :], plt[:], [plb], [], plb)
        allb = [x[1] for x in [KT, V, wout, cst, tri, lpar, subln, gB, bB, ybT, rl, Om, sqa, sdn, eps6]]
        allb += [x[1] for x in qz[0] + qz[1] + yaTg + r + PT + [onesb] + att + stt_ + mv + sd + rs] + KTb + Vb + woutb + ybTb
        allb += [x[1] for x in lam.values()] + [x[1] for x in lsm.values()] + [x[1] for x in B.pbanks]
        if ple_d is not None:
            allb += [x[1] for x in extra] + extra_b
        for en in ("pe", "act", "dve", "pool", "sp"):
            P.wait_all(en, allb)
        P.emit()
```

```python
import math
from contextlib import ExitStack

import numpy as np
import concourse.bass as bass
import concourse.mybir as mybir
from concourse.bass_utils import run_bass_kernel_spmd

F32 = mybir.dt.float32
BF16 = mybir.dt.bfloat16
AF = mybir.ActivationFunctionType
ALU = mybir.AluOpType

D = 1024
FF = 2816
NJ = FF // 128
PLE = 256
INW = 3592
ALPHA = 2.0 ** 0.25
LN_EPS = 1e-5
RMS_EPS = 1e-6
LAM_INIT = 0.8 - 0.6 * math.exp(-0.3 * 0)

SEM_LIMIT = 30000


class Buf:
    __slots__ = ("name", "last_w", "readers", "dsem", "dcount")

    def __init__(self, name):
        self.name = name
        self.last_w = None
        self.readers = []
        self.dsem = None
        self.dcount = 0


class Eng:
    def __init__(self, name):
        self.name = name
        self.items = []
        self.semkey = None
        self.count = 0
        self.known = {}


class Prog:
    def __init__(self, nc, stack):
        self.nc = nc
        self.stack = stack
        self.sems = {}
        self.nsem = 0
        self.free_dsems = []
        self.dma_bufs = []
        self.engs = {n: Eng(n) for n in ("pe", "act", "dve", "pool", "sp")}
        for e in self.engs.values():
            e.semkey = self.new_sem(e.name)

    def new_sem(self, name):
        key = "%s_%d" % (name, self.nsem)
        self.nsem += 1
        self.sems[key] = self.stack.enter_context(self.nc.semaphore(key))
        return key

    def _need(self, eng, toks):
        for t in toks:
            if t is None:
                continue
            k, v = t
            if eng.known.get(k, 0) >= v:
                continue
            if k == eng.semkey and v > eng.count:
                continue
            eng.known[k] = v
            eng.items.append(("wait", k, v))

    def op(self, engname, fn, reads=(), writes=(), inc=True):
        eng = self.engs[engname]
        toks = []
        for b in reads:
            toks.append(b.last_w)
        for b in writes:
            toks.append(b.last_w)
            toks.extend(b.readers)
        self._need(eng, toks)
        if eng.count >= SEM_LIMIT and inc:
            eng.semkey = self.new_sem(eng.name)
            eng.count = 0
        tok = (eng.semkey, eng.count + 1)
        if inc:
            eng.count += 1
            if engname == "pe":
                eng.known[eng.semkey] = eng.count
        eng.items.append(("op", fn, eng.semkey if inc else None, 1))
        for b in reads:
            b.readers.append(tok)
        for b in writes:
            b.last_w = tok
            b.readers = []
        return tok

    def dma(self, qname, out_ap, in_ap, reads, writes, sbuf, **kw):
        eng = self.engs[qname]
        toks = []
        for b in reads:
            toks.append(b.last_w)
        for b in writes:
            toks.append(b.last_w)
            toks.extend(b.readers)
        self._need(eng, toks)
        if qname == "pool":
            assert sbuf.dsem is None
            sbuf.dsem, sbuf.dcount = self.new_sem("w"), 0
        elif sbuf.dsem is None:
            if self.free_dsems:
                sbuf.dsem, sbuf.dcount = self.free_dsems.pop()
            else:
                sbuf.dsem, sbuf.dcount = self.new_sem("d"), 0
            self.dma_bufs.append(sbuf)
        sbuf.dcount += 16
        tok = (sbuf.dsem, sbuf.dcount)

        def fn(e, out_ap=out_ap, in_ap=in_ap, kw=kw):
            return e.dma_start(out=out_ap, in_=in_ap, **kw)

        eng.items.append(("op", fn, sbuf.dsem, 16))
        for b in reads:
            b.readers.append(tok)
        for b in writes:
            b.last_w = tok
            b.readers = []
        return tok

    def wait_all(self, engname, bufs):
        eng = self.engs[engname]
        toks = []
        for b in bufs:
            toks.append(b.last_w)
            toks.extend(b.readers)
        self._need(eng, toks)

    def emit(self):
        nc = self.nc
        with nc.Block() as block:
            def runner(eng):
                def body(e):
                    for it in eng.items:
                        if it[0] == "wait":
                            e.wait_ge(self.sems[it[1]], it[2])
                        else:
                            ins = it[1](e)
                            if it[2] is not None:
                                ins.then_inc(self.sems[it[2]], it[3])
                return body
            block.sync(runner(self.engs["sp"]))
            block.scalar(runner(self.engs["act"]))
            block.vector(runner(self.engs["dve"]))
            block.gpsimd(runner(self.engs["pool"]))
            block.tensor(runner(self.engs["pe"]))
        for e in self.engs.values():
            e.items = []
        for b in self.dma_bufs:
            self.free_dsems.append((b.dsem, b.dcount))
            b.dsem = None
        self.dma_bufs = []


class Builder:
    def __init__(self, nc, stack, cfg):
        self.nc = nc
        self.stack = stack
        self.cfg = cfg
        self.P = Prog(nc, stack)
        self.nbuf = 0

    def sb(self, name, shape, dt):
        t = self.stack.enter_context(self.nc.sbuf_tensor("sb_" + name, list(shape), dt))
        return t, Buf(name)

    def sbn(self, name, shape, dt, n):
        return [self.sb("%s%d" % (name, i), shape, dt) for i in range(n)]

    def ps(self, name, shape, dt):
        t = self.stack.enter_context(self.nc.psum_tensor("ps_" + name, list(shape), dt))
        return t, Buf(name)


    def mm(self, out, ob, lhsT, rhs, reads, start=True, stop=True):
        self.P.op("pe", lambda e: e.matmul(out, lhsT=lhsT, rhs=rhs, start=start, stop=stop),
                  reads=reads, writes=[ob], inc=stop)

    def tr(self, out, ob, in_, reads, inc=True):
        idn = self.ident
        self.P.op("pe", lambda e: e.transpose(out=out, in_=in_, identity=idn[0][:]),
                  reads=list(reads) + [idn[1]], writes=[ob], inc=inc)

    def act(self, out, in_, func, reads, writes, **kw):
        self.P.op("act", lambda e: e.activation(out=out, in_=in_, func=func, **kw), reads=reads, writes=writes)

    def tt(self, eng, out, in0, in1, op, reads, writes):
        self.P.op(eng, lambda e: e.tensor_tensor(out=out, in0=in0, in1=in1, op=op), reads=reads, writes=writes)

    def ts(self, eng, out, in0, s1, s2, op0, op1, reads, writes):
        self.P.op(eng, lambda e: e.tensor_scalar(out=out, in0=in0, scalar1=s1, scalar2=s2, op0=op0, op1=op1),
                  reads=reads, writes=writes)

    def stt(self, out, in0, scalar, in1, op0, op1, reads, writes):
        self.P.op("dve", lambda e: e.scalar_tensor_tensor(out=out, in0=in0, scalar=scalar, in1=in1, op0=op0, op1=op1),
                  reads=reads, writes=writes)

    def cp(self, eng, out, in_, reads, writes):
        if eng == "act":
            self.P.op("act", lambda e: e.copy(out=out, in_=in_), reads=reads, writes=writes)
        else:
            self.P.op(eng, lambda e: e.tensor_copy(out=out, in_=in_), reads=reads, writes=writes)

    def banks(self, st, tag):
        self.pbanks = [self.ps_in(st, "%sbk%d" % (tag, i), [128, 512], F32) for i in range(8)]
        self.bki = 0
        self.rotbanks = None

    def ps_in(self, st, name, shape, dt):
        t = st.enter_context(self.nc.psum_tensor("ps_" + name, list(shape), dt))
        return t, Buf(name)

    def bank(self):
        rot = getattr(self, "rotbanks", None) or list(range(8))
        b = self.pbanks[rot[self.bki % len(rot)]]
        self.bki += 1
        return b

    def dram(self, name, shape, dt, kind):
        return self.nc.dram_tensor(name, list(shape), dt, kind=kind).ap()


def ffn_phase(B, x_src, out_dst, w13_d, w2_d, lng_d, lnb_d, ident_d, ntok, G=512, ple=None, tag="f"):
    nc, P = B.nc, B.P
    TPG = G // 128
    ngroups = ntok // G
    with ExitStack() as st:
        B.stack = st
        NR = 6
        w13 = B.sb(tag + "w13", [128, 8, 2 * FF], BF16)
        w2 = B.sb(tag + "w2", [128, NJ, D], BF16)
        gB = B.sb(tag + "gB", [128, D], F32)
        bB = B.sb(tag + "bB", [128, D], F32)
        ident = B.sb(tag + "ident", [128, 128], F32)
        B.ident = (ident[0][:], ident[1])
        r = B.sbn(tag + "r", [128, D], F32, NR)
        xT = B.sb(tag + "xT", [128, 8, G], BF16)
        hT = B.sb(tag + "hT", [128, NJ, G], BF16)
        xTb = [Buf(tag + "xT%d" % c) for c in range(8)]
        hTb = [Buf(tag + "hT%d" % c) for c in range(NJ)]
        sl = B.sbn(tag + "s", [128, G], F32, 2)
        stt = B.sbn(tag + "st", [128, 2, 6], F32, 2)
        mv = B.sbn(tag + "mv", [128, 2], F32, 2)
        sd = B.sbn(tag + "sd", [128, 1], F32, 2)
        rs = B.sbn(tag + "rs", [128, 1], F32, 2)
        pT = [B.ps(tag + "pT%d" % i, [128, 512], F32) for i in range(2)]
        pg = [B.ps(tag + "pg%d" % i, [128, 512], F32) for i in range(2)]
        pu = [B.ps(tag + "pu%d" % i, [128, 512], F32) for i in range(2)]
        pd = [B.ps(tag + "pd%d" % i, [128, 512], F32) for i in range(2)]
        extra = []
        if ple is not None:
            plebuf = B.sbn(tag + "ple", [128, D], F32, 2)
            extra = plebuf
        P.dma("sp", ident[0][:], ident_d, [], [ident[1]], ident[1])
        P.dma("sp", gB[0][:], lng_d, [], [gB[1]], gB[1])
        P.dma("sp", bB[0][:], lnb_d, [], [bB[1]], bB[1])
        w13v = w13_d.rearrange("(c p) f -> p c f", p=128)
        w2v = w2_d.rearrange("(c p) f -> p c f", p=128)
        w13bufs = [Buf(tag + "w13_%d" % c) for c in range(8)]
        for c in range(8):
            P.dma("pool", w13[0][:, c, :], w13v[:, c, :], [], [w13bufs[c]], w13bufs[c], max_dma_last_dim=8192)
        wgb = []
        w2p = [Buf(tag + "w2_%d" % c) for c in range(3)]
        w2bufs = [w2p[min(c // 4, 2)] for c in range(NJ // 2)]
        for c, (a, b) in enumerate(((0, 8), (8, 16), (16, NJ))):
            P.dma("pool", w2[0][:, a:b, :], w2v[:, a:b, :], [], [w2p[c]], w2p[c], max_dma_last_dim=8192)

        ti = 0
        for g in range(ngroups):
            slots = []
            for tt in range(TPG):
                rt, rb = r[ti % NR]
                ti += 1
                slots.append((rt, rb))
                t0 = g * G + tt * 128
                P.dma("sp", rt[:], x_src[t0:t0 + 128, :], [], [rb], rb)
            for c in range(8):
                pt, pb = pT[c % 2]
                for tt in range(TPG):
                    rt, rb = slots[tt]
                    B.tr(pt[:, tt * 128:(tt + 1) * 128], pb, rt[:, c * 128:(c + 1) * 128], [rb], inc=(tt == TPG - 1))
                B.cp("act" if c % 2 == 0 else "dve", xT[0][:, c, :], pt[:, 0:G], [pb], [xTb[c]])
            for j in range(NJ):
                pgt, pgb = pg[j % 2]
                put, pub = pu[j % 2]
                for c in range(8):
                    B.mm(pgt[:, 0:G], pgb, w13[0][:, c, j * 128:(j + 1) * 128], xT[0][:, c, :], [w13bufs[c], xTb[c]], start=(c == 0), stop=(c == 7))
                for c in range(8):
                    B.mm(put[:, 0:G], pub, w13[0][:, c, FF + j * 128:FF + (j + 1) * 128], xT[0][:, c, :], [w13bufs[c], xTb[c]], start=(c == 0), stop=(c == 7))
                st_, sb_ = sl[j % 2]
                B.act(st_[:], pgt[:, 0:G], AF.Silu, [pgb], [sb_])
                B.stt(hT[0][:, j, :], st_[:], 0.5, put[:, 0:G], ALU.mult, ALU.mult, [sb_, pub], [hTb[j]])
            for tt in range(TPG):
                rt, rb = slots[tt]
                for dh in range(2):
                    pdt, pdb = pd[(tt * 2 + dh) % 2]
                    for j in range(NJ):
                        B.mm(pdt[:], pdb, hT[0][:, j, tt * 128:(tt + 1) * 128], w2[0][:, j, dh * 512:(dh + 1) * 512],
                             [hTb[j], w2bufs[j // 2]], start=(j == 0), stop=(j == NJ - 1))
                    B.stt(rt[:, dh * 512:(dh + 1) * 512], rt[:, dh * 512:(dh + 1) * 512], ALPHA, pdt[:], ALU.mult, ALU.add, [pdb, rb], [rb])
                if ple is not None:
                    plt, plb = plebuf[(g * TPG + tt) % 2]
                    t0 = g * G + tt * 128
                    P.dma("sp", plt[:], ple[t0:t0 + 128, :], [], [plb], plb)
                    B.tt("pool", rt[:], rt[:], plt[:], ALU.add, [rb, plb], [rb])
                k2 = (g * TPG + tt) % 2
                layer_norm(B, rt, rb, stt[k2], mv[k2], sd[k2], rs[k2], gB, bB)
                t0 = g * G + tt * 128
                P.dma("sp", out_dst[t0:t0 + 128, :], rt[:], [rb], [], rb)
        allb = xTb + hTb + [w13[1], w2[1], gB[1], bB[1], ident[1], xT[1], hT[1]] + [x[1] for x in r + sl + stt + mv + sd + rs + pT + pg + pu + pd + extra] + w13bufs + w2bufs + wgb
        for en in ("pe", "act", "dve", "pool", "sp"):
            P.wait_all(en, allb)
        P.emit()


def layer_norm(B, rt, rb, st, mv, sd, rs, gB, bB):
    P = B.P
    for h in range(2):
        P.op("dve", lambda e, o=st[0][:, h, :], i=rt[:, h * 512:(h + 1) * 512]: e.bn_stats(out=o, in_=i),
             reads=[rb], writes=[st[1]])
    P.op("dve", lambda e: e.bn_aggr(out=mv[0][:], in_=st[0][:].rearrange("p a b -> p (a b)")),
         reads=[st[1]], writes=[mv[1]])
    P.op("act", lambda e: e.activation(out=sd[0][:], in_=mv[0][:, 1:2], func=AF.Ln, bias=B.eps_ln[0][:], scale=1.0),
         reads=[mv[1], B.eps_ln[1]], writes=[sd[1]])
    P.op("act", lambda e: e.activation(out=rs[0][:], in_=sd[0][:], func=AF.Exp, scale=-0.5), reads=[sd[1]], writes=[rs[1]])
    P.op("dve", lambda e: e.tensor_scalar(out=rt[:], in0=rt[:], scalar1=mv[0][:, 0:1], scalar2=rs[0][:],
                                          op0=ALU.subtract, op1=ALU.mult),
         reads=[rb, mv[1], rs[1]], writes=[rb])
    P.op("pool", lambda e: e.tensor_tensor(out=rt[:], in0=rt[:], in1=gB[0][:], op=ALU.mult),
         reads=[rb, gB[1]], writes=[rb])
    P.op("pool", lambda e: e.tensor_tensor(out=rt[:], in0=rt[:], in1=bB[0][:], op=ALU.add),
         reads=[rb, bB[1]], writes=[rb])


def build_nc(cfg):
    nc = bass.Bass("TRN2", target_bir_lowering=False)
    nseq, seqlen = cfg["nseq"], cfg["seqlen"]
    ntok = nseq * seqlen
    with ExitStack() as stack:
        B = Builder(nc, stack, cfg)
        di = lambda n, shp, dt=F32: B.dram(n, shp, dt, "ExternalInput")
        x_d = di("x", [ntok, D])
        p_d = di("p", [ntok, PLE])
        ident_d = di("ident", [128, 128])
        cst_d = di("cst", [128, 520])
        tri_d = di("tri", [128, 128])
        rot_d = di("rot", [seqlen // 128, 128, 16])
        w13_1 = di("ffn1_w13", [D, 2 * FF])
        w2_1 = di("ffn1_w2", [FF, D])
        w13_2 = di("ffn2_w13", [D, 2 * FF])
        w2_2 = di("ffn2_w2", [FF, D])
        lng = [di(n, [128, D]) for n in ("ln1_g", "ln2_g", "ln3_g")]
        lnb = [di(n, [128, D]) for n in ("ln1_b", "ln2_b", "ln3_b")]
        w_in_d = di("w_in", [D, INW])
        w_out_d = di("w_out", [D, D])
        convw_d = di("convw", [128, 4, 1536])
        gpar_d = di("gpar", [128, 136])
        lpar_d = di("lpar", [128, 256])
        subln_d = di("subln", [128, 1])
        wg_d = di("ple_gate_w", [D, D])
        wp_d = di("ple_proj_w", [PLE, D])
        out_d = B.dram("out", [ntok, D], F32, "ExternalOutput")
        x1_d = B.dram("x1_scr", [ntok, D], F32, "Internal")
        x2_d = B.dram("x2_scr", [ntok, D], F32, "Internal")
        qT_d = B.dram("qT_scr", [512, ntok], BF16, "Internal")
        kT_d = B.dram("kT_scr", [512, ntok], BF16, "Internal")
        v_d = B.dram("v_scr", [ntok, 512], BF16, "Internal")
        yaT_d = B.dram("yaT_scr", [512, ntok], BF16, "Internal")
        ple_d = B.dram("ple_scr", [ntok, D], F32, "Internal")
        B.eps_ln = B.sb("eps_ln", [128, 1], F32)
        B.P.op("pool", lambda e: e.memset(B.eps_ln[0][:], LN_EPS), writes=[B.eps_ln[1]])
        with nc.allow_low_precision("bf16 matmul operands, fp32 accumulation (problem tolerance calibrated for this)"):
            ffn_phase(B, x_d, x1_d, w13_1, w2_1, lng[0], lnb[0], ident_d, ntok, G=512, tag="f1")
            gdn_phase(B, x1_d, w_in_d, cst_d, convw_d, gpar_d, rot_d, qT_d, kT_d, v_d, yaT_d, nseq, seqlen)
            attn_phase(B, x1_d, qT_d, kT_d, v_d, yaT_d, w_out_d, cst_d, tri_d, lpar_d, subln_d, lng[1], lnb[1], x2_d, nseq, seqlen,
                       p_d=p_d, wg_d=wg_d, wp_d=wp_d, ple_d=ple_d)
            ffn_phase(B, x2_d, out_d, w13_2, w2_2, lng[2], lnb[2], ident_d, ntok, G=512, ple=ple_d, tag="f2")
    return nc


def _cst():
    c = np.zeros((128, 520), np.float32)
    idx = np.arange(128)
    same = (idx[:, None] // 64) == (idx[None, :] // 64)
    c[:, 0:128] = np.eye(128)
    c[:, 128:256] = ((idx[:, None] <= idx[None, :]) & same)
    c[:, 256:384] = ((idx[:, None] > idx[None, :]) & same)
    c[:, 384:512] = 1.0
    c[:, 512] = (idx < 64)
    c[:, 513] = (idx >= 64)
    return c


def _rot(seqlen):
    inv = (500000.0 ** (-np.arange(0, 16, 2, dtype=np.float32) / np.float32(16))).astype(np.float32)
    ang = (np.arange(seqlen, dtype=np.float32)[:, None] * inv[None, :]).astype(np.float32)
    r = np.concatenate([np.cos(ang), np.sin(ang)], -1).astype(np.float32)
    return np.ascontiguousarray(r.reshape(seqlen // 128, 128, 16))


def _rep(v, n=128):
    v = np.asarray(v, np.float32)
    return np.ascontiguousarray(np.broadcast_to(v, (n,) + v.shape))


def host_inputs(inputs, seqlen):
    g = lambda k: np.asarray(inputs[k], np.float32)[0]
    idx = np.arange(128)
    com = {
        "ident": np.eye(128, dtype=np.float32), "cst": _cst(),
        "tri": (idx[None, :] >= idx[:, None]).astype(np.float32), "rot": _rot(seqlen),
        "ffn1_w13": g("ffn1_w13"), "ffn1_w2": g("ffn1_w2"), "ffn2_w13": g("ffn2_w13"), "ffn2_w2": g("ffn2_w2"),
        "w_in": g("w_in"), "w_out": g("w_out"), "ple_gate_w": g("ple_gate_w"), "ple_proj_w": g("ple_proj_w"),
        "convw": _rep(g("gdn_conv_w")),
        "gpar": np.ascontiguousarray(np.concatenate([_rep(g("gdn_a_log")), _rep(g("gdn_dt_bias")), _rep(g("gdn_norm_w"))], 1)),
        "lpar": _rep(np.concatenate([g("diff_lq1"), g("diff_lk1"), g("diff_lq2"), g("diff_lk2")])),
        "subln": np.ascontiguousarray(g("diff_subln_w").reshape(128, 1)),
    }
    for n in ("ln1_g", "ln1_b", "ln2_g", "ln2_b", "ln3_g", "ln3_b"):
        com[n] = _rep(g(n))
    return com


def kernel(**inputs):
    x = np.asarray(inputs["x"], np.float32)
    p = np.asarray(inputs["p"], np.float32)[0]
    bsz, seqlen, _ = x.shape
    ncores = 8
    nseq = bsz // ncores
    com = host_inputs(inputs, seqlen)
    nc = build_nc({"nseq": nseq, "seqlen": seqlen})
    in_maps = []
    for c in range(ncores):
        m = dict(com)
        m["x"] = np.ascontiguousarray(x[c * nseq:(c + 1) * nseq].reshape(nseq * seqlen, D))
        m["p"] = np.ascontiguousarray(p[c * nseq:(c + 1) * nseq].reshape(nseq * seqlen, PLE))
        in_maps.append(m)
    res = run_bass_kernel_spmd(nc, in_maps, core_ids=list(range(ncores)))
    out = np.concatenate([np.asarray(r["out"], np.float32).reshape(nseq, seqlen, D) for r in res.results], axis=0)
    return out


C_ID, C_MU, C_MSL, C_ONES, C_CIND = 0, 128, 256, 384, 512
HR_Z, HR_A, HR_B, HR_Q, HR_K, HR_V = 0, 512, 516, 520, 1032, 1544


def r3(ap, h=4):
    return ap.rearrange("p (h d) -> p h d", h=h)


def bc(ap, n, d):
    return ap.unsqueeze(2).to_broadcast([128, n, d])


def bcm(ap, n, d):
    return ap.unsqueeze(1).to_broadcast([128, n, d])


def gdn_phase(B, x1_d, w_in_d, cst_d, convw_d, gpar_d, rot_d, qT_d, kT_d, v_d, yaT_d, nseq, seqlen):
    nc, P = B.nc, B.P
    NT = seqlen // 128
    with ExitStack() as st:
        B.stack = st
        B.banks(st, "g")
        win = B.sb("win", [128, 8, INW], BF16)
        cst = B.sb("cst", [128, 520], F32)
        B.ident = (cst[0][:, 0:128], cst[1])
        convw = B.sb("convw", [128, 4, 1536], F32)
        gpar = B.sb("gpar", [128, 136], F32)
        rot = B.sbn("rot", [128, 16], F32, 2)
        negA = B.sb("negA", [128, 4], F32)
        eps6 = B.sb("eps6", [128, 1], F32)
        one1 = B.sb("one1", [128, 1], F32)
        r = B.sbn("gr", [128, D], F32, 2)
        x1T = B.sbn("x1T", [128, 8, 128], BF16, 2)
        hq = B.sbn("hq", [128, 1536], F32, 2)
        hr = B.sb("hr", [128, 2056], F32)
        hs = B.sbn("hs", [128, 1536], F32, 2)
        hs_halo = [Buf("hsh0"), Buf("hsh1")]
        cc = B.sb("cc", [128, 1536], F32)
        sm = {n: B.sb("sm_" + n, [128, 8], F32) for n in ("ss", "sd", "rinv", "xa", "ea", "sp", "g", "beta", "be", "so", "sdo", "rio")}
        rt = {n: B.sb("rt_" + n, [128, 16, 8], F32) for n in ("t1", "t2", "t3", "t4")}
        zs = B.sbn("zs", [128, 512], F32, 2)
        vb = B.sbn("vb", [128, 512], BF16, 2)
        qTg = B.sb("qTg", [128, 4, 512], BF16)
        kTg = B.sb("kTg", [128, 4, 512], BF16)
        yTg = B.sb("yTg", [128, 4, 512], BF16)
        gM = B.sb("gM", [128, 4, 384], F32)
        Esm = B.sb("Esm", [128, 16], F32)
        S = B.sb("S", [128, 512], F32)
        T = {n: B.sb("T_" + n, [128, 512], F32) for n in ("decay", "decayT", "qdec", "Af")}
        for n in ("knT", "qnT", "qdecT", "A0", "A1", "B0", "B1", "R0", "R1", "L0", "L1", "qkTm", "bv", "kbg", "kdec", "wT",
                  "vnz0", "vnz1", "Sb"):
            T[n] = B.sb("T_" + n, [128, 512], BF16)
        for a, b_ in (("y", "decay"), ("osb", "decayT"), ("u", "qdec")):
            T[a] = T[b_]

        P.dma("sp", cst[0][:], cst_d, [], [cst[1]], cst[1])
        P.dma("sp", convw[0][:], convw_d, [], [convw[1]], convw[1])
        P.dma("sp", gpar[0][:], gpar_d, [], [gpar[1]], gpar[1])
        winv = w_in_d.rearrange("(c p) f -> p c f", p=128)
        winb = [Buf("win_%d" % c) for c in range(8)]
        for c in range(8):
            P.dma("pool", win[0][:, c, :], winv[:, c, :], [], [winb[c]], winb[c], max_dma_last_dim=8192)
        P.op("pool", lambda e: e.memset(eps6[0][:], RMS_EPS), writes=[eps6[1]])
        P.op("pool", lambda e: e.memset(one1[0][:], 1.0), writes=[one1[1]])
        B.act(negA[0][:], gpar[0][:, 0:4], AF.Exp, [gpar[1]], [negA[1]])
        B.ts("dve", negA[0][:], negA[0][:], -1.0, None, ALU.mult, ALU.bypass, [negA[1]], [negA[1]])
        dtb = gpar[0][:, 4:8]
        normw = gpar[0][:, 8:136]
        MU = cst[0][:, C_MU:C_MU + 128]
        MSL = cst[0][:, C_MSL:C_MSL + 128]
        IDN = cst[0][:, C_ID:C_ID + 128]
        ONEC = cst[0][:, C_ONES:C_ONES + 1]
        CIND = cst[0][:, C_CIND:C_CIND + 2]
        cb = cst[1]
        AX = mybir.AxisListType.X

        def hsl(h):
            return slice(h * 128, (h + 1) * 128)

        qn = cc[0][:, 0:512]
        kn = cc[0][:, 512:1024]
        vv = cc[0][:, 1024:1536]
        gg = sm["g"][0][:, 0:4]
        beta = sm["beta"][0][:, 0:4]
        eG = Esm[0][:, 0:4]
        E2 = Esm[0][:, 4:8]

        def F(s, t):
            tok0 = s * seqlen + t * 128
            g4 = t % 4
            rtile, rb = r[t % 2]
            xT, xTb = x1T[t % 2]
            hqc, hqcb = hq[t % 2]
            hqp, hqpb = hq[(t + 1) % 2]
            rott, rotb = rot[t % 2]
            zst, zsb = zs[t % 2]
            P.dma("sp", rtile[:], x1_d[tok0:tok0 + 128, :], [], [rb], rb)
            P.dma("sp", rott[:], rot_d[t], [], [rotb], rotb)
            for half in range(2):
                pt, pb = B.bank()
                for cc_ in range(4):
                    c = half * 4 + cc_
                    B.tr(pt[:, cc_ * 128:(cc_ + 1) * 128], pb, rtile[:, c * 128:(c + 1) * 128], [rb], inc=(cc_ == 3))
                B.cp("act" if half == 0 else "dve", xT[:, half * 4:half * 4 + 4, :], r3(pt[:]), [pb], [xTb])
                yield
            for gi in range(8):
                c0 = gi * 512
                n = min(512, INW - c0)
                pt, pb = B.bank()
                for c in range(8):
                    B.mm(pt[:, 0:n], pb, xT[:, c, :], win[0][:, c, c0:c0 + n], [xTb, winb[c]], start=(c == 0), stop=(c == 7))
                if gi < 3:
                    B.cp("act" if gi % 2 == 0 else "dve", hqc[:, c0:c0 + n], pt[:, 0:n], [pb], [hqcb])
                else:
                    B.cp("act" if gi % 2 == 0 else "dve", hr[0][:, c0 - 1536:c0 - 1536 + n], pt[:, 0:n], [pb], [hr[1]])
                yield
            B.tt("pool", cc[0][:], hqc[:], convw[0][:, 3, :], ALU.mult, [hqcb, convw[1]], [cc[1]])
            for sft in (1, 2, 3):
                k = sft % 2
                hst, hsb = hs[k]
                P.dma("sp", hst[sft:128, :], hqc[0:128 - sft, :], [hqcb], [hsb], hsb)
                if t == 0:
                    P.op("pool", lambda e, o=hst[0:sft, :]: e.memset(o, 0.0), writes=[hs_halo[k]])
                else:
                    P.dma("sp", hst[0:sft, :], hqp[128 - sft:128, :], [hqpb], [hs_halo[k]], hs_halo[k])
                B.tt("pool", hst[:], hst[:], convw[0][:, 3 - sft, :], ALU.mult, [hsb, hs_halo[k], convw[1]], [hsb, hs_halo[k]])
                B.tt("dve", cc[0][:], cc[0][:], hst[:], ALU.add, [cc[1], hsb, hs_halo[k]], [cc[1]])
                yield
            scr, scrb = hs[1]
            B.act(scr[:], cc[0][:], AF.Exp, [cc[1]], [scrb, hs_halo[1]], scale=-1.0)
            B.act(scr[:], scr[:], AF.Ln, [scrb, one1[1]], [scrb], bias=one1[0][:], scale=1.0)
            B.act(scr[:], scr[:], AF.Exp, [scrb], [scrb], scale=-1.0)
            B.tt("dve", cc[0][:], cc[0][:], scr[:], ALU.mult, [cc[1], scrb], [cc[1]])
            B.tt("dve", sm["xa"][0][:, 0:4], hr[0][:, HR_A:HR_A + 4], dtb, ALU.add, [hr[1], gpar[1]], [sm["xa"][1]])
            B.act(sm["ea"][0][:, 0:4], sm["xa"][0][:, 0:4], AF.Exp, [sm["xa"][1]], [sm["ea"][1]])
            B.act(sm["sp"][0][:, 0:4], sm["ea"][0][:, 0:4], AF.Ln, [sm["ea"][1], one1[1]], [sm["sp"][1]], bias=one1[0][:], scale=1.0)
            B.tt("dve", gg, sm["sp"][0][:, 0:4], negA[0][:], ALU.mult, [sm["sp"][1], negA[1]], [sm["g"][1]])
            B.act(beta, hr[0][:, HR_B:HR_B + 4], AF.Exp, [hr[1]], [sm["beta"][1]], scale=-1.0)
            B.act(beta, beta, AF.Ln, [sm["beta"][1], one1[1]], [sm["beta"][1]], bias=one1[0][:], scale=1.0)
            B.act(beta, beta, AF.Exp, [sm["beta"][1]], [sm["beta"][1]], scale=-1.0)
            B.act(zst[:], hr[0][:, HR_Z:HR_Z + 512], AF.Exp, [hr[1]], [zsb], scale=-1.0)
            B.act(zst[:], zst[:], AF.Ln, [zsb, one1[1]], [zsb], bias=one1[0][:], scale=1.0)
            B.act(zst[:], zst[:], AF.Exp, [zsb], [zsb], scale=-1.0)
            B.tt("pool", zst[:], zst[:], hr[0][:, HR_Z:HR_Z + 512], ALU.mult, [zsb, hr[1]], [zsb])
            yield
            sq = (hs[0][0][:, 0:1024], hs[0][1])
            B.tt("pool", sq[0], cc[0][:, 0:1024], cc[0][:, 0:1024], ALU.mult, [cc[1]], [sq[1], hs_halo[0]])
            P.op("dve", lambda e: e.tensor_reduce(out=sm["ss"][0][:], in_=r3(sq[0], 8), axis=AX, op=ALU.add),
                 reads=[sq[1]], writes=[sm["ss"][1]])
            B.act(sm["sd"][0][:], sm["ss"][0][:], AF.Ln, [sm["ss"][1], eps6[1]], [sm["sd"][1]], bias=eps6[0][:], scale=1.0)
            B.act(sm["rinv"][0][:], sm["sd"][0][:], AF.Exp, [sm["sd"][1]], [sm["rinv"][1]], scale=-0.5)
            B.ts("dve", sm["rinv"][0][:, 0:4], sm["rinv"][0][:, 0:4], 128.0 ** -0.5, None, ALU.mult, ALU.bypass,
                 [sm["rinv"][1]], [sm["rinv"][1]])
            B.tt("dve", r3(cc[0][:, 0:1024], 8), r3(cc[0][:, 0:1024], 8), bc(sm["rinv"][0][:], 8, 128), ALU.mult,
                 [cc[1], sm["rinv"][1]], [cc[1]])
            yield
            qk3 = hr[0][:, HR_Q:HR_Q + 1024].rearrange("p (m d) -> p m d", m=16)
            x1v, x2v = qk3[:, :, 0:8], qk3[:, :, 8:16]
            cosb, sinb = bcm(rott[:, 0:8], 16, 8), bcm(rott[:, 8:16], 16, 8)
            rtb = [rt[n][1] for n in ("t1", "t2", "t3", "t4")]
            B.tt("pool", rt["t1"][0][:], x1v, cosb, ALU.mult, [hr[1], rotb], [rtb[0]])
            B.tt("pool", rt["t2"][0][:], x2v, sinb, ALU.mult, [hr[1], rotb], [rtb[1]])
            B.tt("pool", rt["t3"][0][:], x2v, cosb, ALU.mult, [hr[1], rotb], [rtb[2]])
            B.tt("pool", rt["t4"][0][:], x1v, sinb, ALU.mult, [hr[1], rotb], [rtb[3]])
            B.tt("pool", x1v, rt["t1"][0][:], rt["t2"][0][:], ALU.subtract, [rtb[0], rtb[1], hr[1]], [hr[1]])
            B.tt("pool", x2v, rt["t3"][0][:], rt["t4"][0][:], ALU.add, [rtb[2], rtb[3], hr[1]], [hr[1]])
            yield
            tc0 = g4 * 128
            pt, pb = B.bank()
            for hc in range(4):
                B.tr(pt[:, hsl(hc)], pb, hr[0][:, HR_Q + hc * 128:HR_Q + (hc + 1) * 128], [hr[1]], inc=(hc == 3))
            B.act(qTg[0][:, :, tc0:tc0 + 128], r3(pt[:]), AF.Copy, [pb], [qTg[1]], scale=0.125)
            pt, pb = B.bank()
            for hc in range(4):
                B.tr(pt[:, hsl(hc)], pb, hr[0][:, HR_K + hc * 128:HR_K + (hc + 1) * 128], [hr[1]], inc=(hc == 3))
            B.cp("dve", kTg[0][:, :, tc0:tc0 + 128], r3(pt[:]), [pb], [kTg[1]])
            vbt, vbb = vb[t % 2]
            B.cp("act", vbt[:], hr[0][:, HR_V:HR_V + 512], [hr[1]], [vbb])
            P.dma("sp", v_d[tok0:tok0 + 128, :], vbt[:], [vbb], [], vbb)
            if g4 == 3:
                gt0 = tok0 - 384
                for (src, dst) in ((qTg, qT_d), (kTg, kT_d)):
                    P.dma("sp", dst.rearrange("(h p) t -> p h t", p=128)[:, :, gt0:gt0 + 512], src[0][:], [src[1]], [], src[1])
            yield

        def M(s, t):
            B.tt("dve", gM[0][:], bcm(cst[0][:, C_MU:C_MU + 384], 4, 384), bc(gg, 4, 384), ALU.mult, [cb, sm["g"][1]], [gM[1]])
            pE0, pE0b = B.bank()
            pE1, pE1b = B.bank()
            pE2, pE2b = B.bank()
            for h in range(4):
                gU = gM[0][:, h, 0:128]
                gSL = gM[0][:, h, 128:256]
                gON = gM[0][:, h, 256:384]
                B.mm(pE0[:, hsl(h)], pE0b, gU, MSL, [gM[1], cb])
                B.mm(pE1[:, hsl(h)], pE1b, MSL, gU, [gM[1], cb])
                B.mm(pE2[:, h:h + 1], pE2b, gU, ONEC, [gM[1], cb])
                B.mm(pE2[:, 4 + h:5 + h], pE2b, gSL, ONEC, [gM[1], cb])
                B.mm(pE2[:, 8 + h:9 + h], pE2b, gON, CIND[:, 0:1], [gM[1], cb])
                B.mm(pE2[:, 12 + h:13 + h], pE2b, gON, CIND[:, 1:2], [gM[1], cb])
            dec, decb = T["decay"]
            decT, decTb = T["decayT"]
            B.act(Esm[0][:], pE2[:, 0:16], AF.Exp, [pE2b], [Esm[1]])
            B.act(dec[:], pE0[:], AF.Exp, [pE0b], [decb])
            B.act(decT[:], pE1[:], AF.Exp, [pE1b], [decTb])
            qd, qdb = T["qdec"]
            B.tt("dve", r3(qd[:]), r3(qn), bc(eG, 4, 128), ALU.mult, [cc[1], Esm[1]], [qdb])
            for (src, srcb, dstn, eng) in ((kn, cc[1], "knT", "act"), (qn, cc[1], "qnT", "dve"), (qd[:], qdb, "qdecT", "act")):
                pt, pb = B.bank()
                for h in range(4):
                    B.tr(pt[:, hsl(h)], pb, src[:, hsl(h)], [srcb], inc=(h == 3))
                B.cp(eng, T[dstn][0][:], pt[:], [pb], [T[dstn][1]])
            B.tt("pool", r3(dec[:]), r3(dec[:]), bcm(MSL, 4, 128), ALU.mult, [decb, cb], [decb])
            B.tt("pool", r3(dec[:]), r3(dec[:]), bc(beta, 4, 128), ALU.mult, [decb, sm["beta"][1]], [decb])
            B.tt("pool", r3(decT[:]), r3(decT[:]), bcm(MU, 4, 128), ALU.mult, [decTb, cb], [decTb])
            knT, knTb = T["knT"]
            qnT, qnTb = T["qnT"]
            pt, pb = B.bank()
            for h in range(4):
                B.mm(pt[:, hsl(h)], pb, knT[:, hsl(h)], knT[:, hsl(h)], [knTb])
            Af, Afb = T["Af"]
            A, Ab = T["A0"]
            B.tt("dve", Af[:], pt[:], dec[:], ALU.mult, [pb, decb], [Afb])
            pt2, pb2 = B.bank()
            for h in range(4):
                B.mm(pt2[:, hsl(h)], pb2, knT[:, hsl(h)], qnT[:, hsl(h)], [knTb, qnTb])
            pt, pb = B.bank()
            for h in range(4):
                B.tr(pt[:, hsl(h)], pb, Af[:, hsl(h)], [Afb], inc=(h == 3))
            B.cp("pool", A[:], Af[:], [Afb], [Ab])
            B.tt("dve", sm["be"][0][:, 0:4], beta, eG, ALU.mult, [sm["beta"][1], Esm[1]], [sm["be"][1]])
            bv, bvb = T["bv"]
            kbg, kbgb = T["kbg"]
            kdec, kdecb = T["kdec"]
            B.tt("pool", r3(bv[:]), r3(vv), bc(beta, 4, 128), ALU.mult, [cc[1], sm["beta"][1]], [bvb])
            B.tt("pool", r3(kbg[:]), r3(kn), bc(sm["be"][0][:, 0:4], 4, 128), ALU.mult, [cc[1], sm["be"][1]], [kbgb])
            B.tt("pool", r3(kdec[:]), r3(kn), bc(E2, 4, 128), ALU.mult, [cc[1], Esm[1]], [kdecb])
            Bm, Bb = T["B0"]
            B.cp("act", Bm[:], pt[:], [pb], [Bb])
            qkTm, qkTmb = T["qkTm"]
            B.tt("dve", qkTm[:], pt2[:], decT[:], ALU.mult, [pb2, decTb], [qkTmb])
            R, Rb = T["R0"]
            L, Lb = T["L0"]
            B.stt(r3(L[:]), r3(Af[:]), -1.0, bcm(IDN, 4, 128), ALU.mult, ALU.add, [Afb, cb], [Lb])
            B.stt(r3(R[:]), r3(pt[:]), -1.0, bcm(IDN, 4, 128), ALU.mult, ALU.add, [pb, cb], [Rb])

        def C(s, t):
            tok0 = s * seqlen + t * 128
            g4 = t % 4
            tc0 = g4 * 128
            zst, zsb = zs[t % 2]
            A, Ab = T["A0"]
            Bm, Bb = T["B0"]
            R, Rb = T["R0"]
            L, Lb = T["L0"]
            qdT, qdTb = T["qdecT"]
            qkTm, qkTmb = T["qkTm"]
            bv, bvb = T["bv"]
            kbg, kbgb = T["kbg"]
            kdec, kdecb = T["kdec"]
            def squares(k, A, Ab, Bm, Bb):
                An, Anb = T["A%d" % (k % 2)]
                Bn, Bnb = T["B%d" % (k % 2)]
                ptB, pbB = B.bank()
                for h in range(4):
                    B.mm(ptB[:, hsl(h)], pbB, A[:, hsl(h)], Bm[:, hsl(h)], [Ab, Bb])
                ptA = pbA = None
                if k < 5:
                    ptA, pbA = B.bank()
                    for h in range(4):
                        B.mm(ptA[:, hsl(h)], pbA, Bm[:, hsl(h)], A[:, hsl(h)], [Ab, Bb])
                return (An, Anb, Bn, Bnb, ptA, pbA, ptB, pbB)

            def evac_squares(k, sqr):
                An, Anb, Bn, Bnb, ptA, pbA, ptB, pbB = sqr
                B.cp("act", Bn[:], ptB[:], [pbB], [Bnb])
                if k < 5:
                    B.cp("dve", An[:], ptA[:], [pbA], [Anb])

            sqr = squares(1, A, Ab, Bm, Bb)
            evac_squares(1, sqr)
            yield
            for k in range(1, 6):
                An, Anb, Bn, Bnb = sqr[0], sqr[1], sqr[2], sqr[3]
                Rn, Rnb = T["R%d" % (k % 2)]
                Ln, Lnb = T["L%d" % (k % 2)]
                ptR, pbR = B.bank()
                for h in range(4):
                    B.mm(ptR[:, hsl(h)], pbR, L[:, hsl(h)], Bn[:, hsl(h)], [Lb, Bnb])
                if k < 5:
                    ptL, pbL = B.bank()
                    for h in range(4):
                        B.mm(ptL[:, hsl(h)], pbL, R[:, hsl(h)], An[:, hsl(h)], [Rb, Anb])
                    nsqr = squares(k + 1, An, Anb, Bn, Bnb)
                B.tt("dve", Rn[:], ptR[:], R[:], ALU.add, [pbR, Rb], [Rnb])
                if k < 5:
                    B.tt("dve", Ln[:], ptL[:], L[:], ALU.add, [pbL, Lb], [Lnb])
                    evac_squares(k + 1, nsqr)
                    sqr = nsqr
                A, Ab, Bm, Bb, R, Rb, L, Lb = An, Anb, Bn, Bnb, Rn, Rnb, Ln, Lnb
                yield
            pt, pb = B.bank()
            for h in range(4):
                B.mm(pt[:, hsl(h)], pb, R[:, hsl(h)], bv[:, hsl(h)], [Rb, bvb])
            pt2, pb2 = B.bank()
            for h in range(4):
                B.mm(pt2[:, hsl(h)], pb2, kbg[:, hsl(h)], R[:, hsl(h)], [Rb, kbgb])
            u, ub = T["u"]
            B.cp("act", u[:], pt[:], [pb], [ub])
            wT, wTb = T["wT"]
            B.cp("dve", wT[:], pt2[:], [pb2], [wTb])
            yield
            osb, osbb = T["osb"]
            Sb, Sbb = T["Sb"]
            for c in range(2):
                rows = slice(64 * c, 64 * c + 64)
                vn, vnb = T["vnz%d" % c]
                pw, pwb = B.bank()
                for h in range(4):
                    B.mm(pw[:, hsl(h)], pwb, wT[:, hsl(h)], Sb[:, hsl(h)], [wTb, Sbb])
                B.tt("dve", vn[rows, :], u[rows, :], pw[rows, :], ALU.subtract, [ub, pwb], [vnb])
                yield
                po, pob = B.bank()
                for h in range(4):
                    B.mm(po[:, hsl(h)], pob, qdT[:, hsl(h)], Sb[:, hsl(h)], [qdTb, Sbb], start=True, stop=False)
                    B.mm(po[:, hsl(h)], pob, qkTm[:, hsl(h)], vn[:, hsl(h)], [qkTmb, vnb], start=False, stop=True)
                pS, pSb = B.bank()
                for h in range(4):
                    B.mm(pS[:, hsl(h)], pSb, kdec[:, hsl(h)], vn[:, hsl(h)], [kdecb, vnb])
                B.cp("act", osb[rows, :], po[rows, :], [pob], [osbb])
                eGl = Esm[0][:, 8 + 4 * c:12 + 4 * c]
                B.tt("dve", r3(S[0][:]), r3(S[0][:]), bc(eGl, 4, 128), ALU.mult, [S[1], Esm[1]], [S[1]])
                B.tt("dve", S[0][:], S[0][:], pS[:], ALU.add, [S[1], pSb], [S[1]])
                B.cp("act", Sb[:], S[0][:], [S[1]], [Sbb])
                yield
            y, yb = T["y"]
            B.tt("pool", y[:], osb[:], osb[:], ALU.mult, [osbb], [yb])
            P.op("dve", lambda e: e.tensor_reduce(out=sm["so"][0][:, 0:4], in_=r3(y[:]), axis=AX, op=ALU.add),
                 reads=[yb], writes=[sm["so"][1]])
            B.act(sm["sdo"][0][:, 0:4], sm["so"][0][:, 0:4], AF.Ln, [sm["so"][1], eps6[1]], [sm["sdo"][1]], bias=eps6[0][:], scale=1.0 / 128.0)
            B.act(sm["rio"][0][:, 0:4], sm["sdo"][0][:, 0:4], AF.Exp, [sm["sdo"][1]], [sm["rio"][1]], scale=-0.5)
            yield
            B.tt("dve", r3(y[:]), r3(osb[:]), bc(sm["rio"][0][:, 0:4], 4, 128), ALU.mult, [osbb, sm["rio"][1], yb], [yb])
            B.tt("pool", r3(y[:]), r3(y[:]), bcm(normw, 4, 128), ALU.mult, [yb, gpar[1]], [yb])
            B.tt("pool", y[:], y[:], zst[:], ALU.mult, [yb, zsb], [yb])
            yield
            pt, pb = B.bank()
            for h in range(4):
                B.tr(pt[:, hsl(h)], pb, y[:, hsl(h)], [yb], inc=(h == 3))
            B.cp("act", yTg[0][:, :, tc0:tc0 + 128], r3(pt[:]), [pb], [yTg[1]])
            if g4 == 3:
                gt0 = tok0 - 384
                P.dma("sp", yaT_d.rearrange("(h p) t -> p h t", p=128)[:, :, gt0:gt0 + 512], yTg[0][:], [yTg[1]], [], yTg[1])
            yield

        for s in range(nseq):
            P.op("pool", lambda e: e.memset(S[0][:], 0.0), writes=[S[1]])
            P.op("pool", lambda e: e.memset(T["Sb"][0][:], 0.0), writes=[T["Sb"][1]])
            if s == 0:
                for c in range(2):
                    P.op("pool", lambda e, o=T["vnz%d" % c][0][:]: e.memset(o, 0.0), writes=[T["vnz%d" % c][1]])
            for _ in F(s, 0):
                pass
            for t in range(NT):
                M(s, t)
                gens = [C(s, t)]
                if t + 1 < NT:
                    gens.append(F(s, t + 1))
                while gens:
                    for gen in list(gens):
                        try:
                            next(gen)
                        except StopIteration:
                            gens.remove(gen)
        allb = [x[1] for x in [win, cst, convw, gpar, negA, eps6, one1, hr, cc, qTg, kTg, yTg, gM, Esm, S]]
        allb += [x[1] for x in rot + r + x1T + hq + hs + vb + zs] + hs_halo + winb
        allb += [x[1] for x in sm.values()] + [x[1] for x in rt.values()] + [x[1] for x in T.values()] + [x[1] for x in B.pbanks]
        for en in ("pe", "act", "dve", "pool", "sp"):
            P.wait_all(en, allb)
        P.emit()


def attn_phase(B, x1_d, qT_d, kT_d, v_d, yaT_d, wout_d, cst_d, tri_d, lpar_d, subln_d, lng_d, lnb_d, x2_d, nseq, seqlen,
               p_d=None, wg_d=None, wp_d=None, ple_d=None):
    nc, P = B.nc, B.P
    NT = seqlen // 128
    NG = seqlen // 512
    NPC = max(1, NT // 8)
    with ExitStack() as st:
        B.stack = st
        B.banks(st, "a")
        B.rotbanks = [4, 5, 6, 7]
        pOs = [B.pbanks[0], B.pbanks[1]]
        pLs = [B.pbanks[2], B.pbanks[3]]
        KT = B.sb("KT", [128, 4, seqlen], BF16)
        V = B.sb("V", [128, NT, 512], BF16)
        KTb = [Buf("KT%d" % i) for i in range(NPC)]
        Vb = [Buf("V%d" % i) for i in range(NPC)]
        wout = B.sb("wout", [128, 8, D], BF16)
        cst = B.sb("acst", [128, 520], F32)
        B.ident = (cst[0][:, 0:128], cst[1])
        ONES = cst[0][:, C_ONES:C_ONES + 128]
        tri = B.sb("tri", [128, 128], F32)
        lpar = B.sb("lpar", [128, 256], F32)
        subln = B.sb("subln", [128, 1], F32)
        gB = B.sb("agB", [128, D], F32)
        bB = B.sb("abB", [128, D], F32)
        lam = {n: B.sb("lam_" + n, [128, 64], F32) for n in ("p1", "p2")}
        lsm = {n: B.sb("lsm_" + n, [128, 1], F32) for n in ("s1", "s2", "e1", "e2", "nl")}
        qz = [B.sbn("aqz%d_" % m, [128, 4, 512], BF16, 2) for m in range(2)]
        yaTg = B.sbn("ayaTg", [128, 4, 512], BF16, 2)
        ybT = B.sb("ybT", [128, 4, 512], BF16)
        ybTb = [Buf("ybT%d" % h) for h in range(4)]
        NR = 6
        r = B.sbn("ar", [128, D], F32, NR)
        PT = B.sbn("PT", [128, 512], BF16, 3)
        onesb = B.sb("onesb", [128, 128], BF16)
        rl = B.sb("rl", [128, 512], F32)
        Om = B.sb("Om", [128, 512], F32)
        att = B.sbn("att", [128, 512], F32, 2)
        sqa = B.sb("sqa", [128, 512], F32)
        sdn = B.sb("sdn", [128, 512], F32)
        eps6 = B.sb("aeps6", [128, 1], F32)
        stt_ = B.sbn("ast", [128, 2, 6], F32, 2)
        mv = B.sbn("amv", [128, 2], F32, 2)
        sd = B.sbn("asd", [128, 1], F32, 2)
        rs = B.sbn("ars", [128, 1], F32, 2)

        extra = []
        if ple_d is not None:
            wg = B.sb("awg", [128, 8, D], BF16)
            wp = B.sb("awp", [128, 2, D], BF16)
            x2T = B.sbn("ax2T", [128, 8, 128], BF16, 2)
            plebuf = B.sbn("aple", [128, D], F32, 2)
            ptile = B.sbn("aptile", [128, PLE], F32, 2)
            pTt = B.sbn("apTt", [128, 2, 128], BF16, 2)
            sg = B.sb("asg", [128, 512], F32)
            extra = [wg, wp, sg] + x2T + plebuf + ptile + pTt
            wgv = wg_d.rearrange("(c p) f -> p c f", p=128)
            wpv = wp_d.rearrange("(c p) f -> p c f", p=128)
            wgp = [Buf("awg_%d" % c) for c in range(3)]
            wgb = [wgp[0], wgp[0], wgp[1], wgp[1], wgp[2]]
            for c in range(2):
                P.dma("pool", wg[0][:, 4 * c:4 * c + 4, :], wgv[:, 4 * c:4 * c + 4, :], [], [wgp[c]], wgp[c], max_dma_last_dim=8192)
            P.dma("pool", wp[0][:], wpv, [], [wgp[2]], wgp[2], max_dma_last_dim=8192)
            extra_b = wgp
        P.dma("sp", cst[0][:], cst_d, [], [cst[1]], cst[1])
        P.dma("sp", tri[0][:], tri_d, [], [tri[1]], tri[1])
        P.dma("sp", lpar[0][:], lpar_d, [], [lpar[1]], lpar[1])
        P.dma("sp", subln[0][:], subln_d, [], [subln[1]], subln[1])
        P.dma("sp", gB[0][:], lng_d, [], [gB[1]], gB[1])
        P.dma("sp", bB[0][:], lnb_d, [], [bB[1]], bB[1])
        woutv = wout_d.rearrange("(c p) f -> p c f", p=128)
        woutp = [Buf("wout_%d" % c) for c in range(2)]
        woutb = [woutp[c // 2] for c in range(4)]
        for c in range(2):
            P.dma("pool", wout[0][:, 4 * c:4 * c + 4, :], woutv[:, 4 * c:4 * c + 4, :], [], [woutp[c]], woutp[c], max_dma_last_dim=8192)
        P.op("pool", lambda e: e.memset(eps6[0][:], RMS_EPS), writes=[eps6[1]])
        P.op("pool", lambda e: e.memset(onesb[0][:], 1.0), writes=[onesb[1]])
        for m in range(2):
            for i in range(2):
                P.op("pool", lambda e, o=qz[m][i][0][64 * (1 - m):64 * (1 - m) + 64, :, :]: e.memset(o, 0.0), writes=[qz[m][i][1]])
        for i, (pn, sn, en) in enumerate((("p1", "s1", "e1"), ("p2", "s2", "e2"))):
            B.tt("dve", lam[pn][0][:], lpar[0][:, 128 * i:128 * i + 64], lpar[0][:, 128 * i + 64:128 * i + 128], ALU.mult, [lpar[1]], [lam[pn][1]])
            P.op("dve", lambda e, o=lsm[sn][0][:], i_=lam[pn][0][:]: e.tensor_reduce(out=o, in_=i_, axis=mybir.AxisListType.X, op=ALU.add),
                 reads=[lam[pn][1]], writes=[lsm[sn][1]])
            B.act(lsm[en][0][:], lsm[sn][0][:], AF.Exp, [lsm[sn][1]], [lsm[en][1]])
        B.tt("dve", lsm["nl"][0][:], lsm["e2"][0][:], lsm["e1"][0][:], ALU.subtract, [lsm["e1"][1], lsm["e2"][1]], [lsm["nl"][1]])
        B.ts("dve", lsm["nl"][0][:], lsm["nl"][0][:], -LAM_INIT, None, ALU.add, ALU.bypass, [lsm["nl"][1]], [lsm["nl"][1]])
        neglam = lsm["nl"]

        kTv = kT_d.rearrange("(h p) t -> p h t", p=128)
        qTv = qT_d.rearrange("(h p) t -> p h t", p=128)
        yaTv = yaT_d.rearrange("(h p) t -> p h t", p=128)
        ti = 0
        for s in range(nseq):
            s0 = s * seqlen
            for i in range(NPC):
                n = seqlen // NPC
                P.dma("sp", KT[0][:, :, i * n:(i + 1) * n], kTv[:, :, s0 + i * n:s0 + (i + 1) * n], [], [KTb[i]], KTb[i])
                nt = NT // NPC
                P.dma("sp", V[0][:, i * nt:(i + 1) * nt, :],
                      v_d[s0 + i * n:s0 + (i + 1) * n, :].rearrange("(t p) f -> p t f", p=128), [], [Vb[i]], Vb[i])
            for g in range(NG):
                g0 = s0 + g * 512
                yg, ygb = yaTg[g % 2]
                qzm = []
                for m in range(2):
                    qzt, qzb = qz[m][g % 2]
                    P.dma("sp", qzt[64 * m:64 * m + 64, :, :], qTv[64 * m:64 * m + 64, :, g0:g0 + 512], [], [qzb], qzb)
                    qzm.append((qzt, qzb))
                P.dma("sp", yg[:], yaTv[:, :, g0:g0 + 512], [], [ygb], ygb)
                slots = []
                for tt in range(4):
                    rt_, rb = r[ti % NR]
                    ti += 1
                    slots.append((rt_, rb))
                    P.dma("sp", rt_[:], x1_d[g0 + tt * 128:g0 + (tt + 1) * 128, :], [], [rb], rb)
                nkb = 4 * (g + 1)
                pending = []
                for h in range(4):
                    at, atb = att[h % 2]
                    for m in range(2):
                        rows = slice(64 * m, 64 * m + 64)
                        pO, pOb = pOs[m]
                        pL, pLb = pLs[m]
                        LA = 2
                        pSq = {}

                        def issue_scores(kb_):
                            j_ = kb_ - 4 * g
                            q0_ = 128 * j_ if j_ > 0 else 0
                            pS_, pSb_ = B.bank()
                            pc_ = min(kb_ // 8, NPC - 1) if NPC > 1 else 0
                            B.mm(pS_[:, q0_:512], pSb_, KT[0][:, h, kb_ * 128:(kb_ + 1) * 128], qzm[m][0][:, h, q0_:512], [KTb[pc_], qzm[m][1]])
                            pSq[kb_] = (pS_, pSb_)

                        for kb_ in range(min(LA, nkb)):
                            issue_scores(kb_)
                        for kb in range(nkb):
                            j = kb - 4 * g
                            q0 = 128 * j if j > 0 else 0
                            if kb + LA < nkb:
                                issue_scores(kb + LA)
                            pS, pSb = pSq.pop(kb)
                            pc = min(kb // 8, NPC - 1) if NPC > 1 else 0
                            pt_, ptb = PT[kb % 3]
                            B.act(pt_[:, q0:512], pS[:, q0:512], AF.Exp, [pSb], [ptb])
                            if j >= 0:
                                B.tt("pool", pt_[:, q0:q0 + 128], pt_[:, q0:q0 + 128], tri[0][:], ALU.mult, [ptb, tri[1]], [ptb])
                            B.mm(pO[:, q0:512], pOb, V[0][:, kb, h * 128:(h + 1) * 128], pt_[:, q0:512], [Vb[pc], ptb],
                                 start=(kb == 0), stop=(kb == nkb - 1))
                            B.mm(pL[:, q0:512], pLb, onesb[0][:], pt_[:, q0:512], [onesb[1], ptb],
                                 start=(kb == 0), stop=(kb == nkb - 1))
                        B.act(rl[0][:], pL[:], AF.Ln, [pLb], [rl[1]])
                        B.act(rl[0][:], rl[0][:], AF.Exp, [rl[1]], [rl[1]], scale=-1.0)
                        if m == 0:
                            while pending:
                                pending.pop(0)()
                            B.tt("dve", at[:], pO[:], rl[0][:], ALU.mult, [pOb, rl[1]], [atb])
                        else:
                            B.tt("dve", Om[0][:], pO[:], rl[0][:], ALU.mult, [pOb, rl[1]], [Om[1]])
                            B.stt(at[:], Om[0][:], neglam[0][:], at[:], ALU.mult, ALU.add, [Om[1], neglam[1], atb], [atb])
                    def head_epilogue(h=h, at=at, atb=atb):
                        B.tt("pool", sqa[0][:], at[:], at[:], ALU.mult, [atb], [sqa[1]])
                        pN, pNb = B.bank()
                        B.mm(pN[:], pNb, ONES, sqa[0][:], [cst[1], sqa[1]])
                        B.act(sdn[0][:], pN[:], AF.Ln, [pNb, eps6[1]], [sdn[1]], bias=eps6[0][:], scale=1.0 / 128.0)
                        B.act(sdn[0][:], sdn[0][:], AF.Exp, [sdn[1]], [sdn[1]], scale=-0.5)
                        B.tt("dve", at[:], at[:], sdn[0][:], ALU.mult, [atb, sdn[1]], [atb])
                        B.ts("dve", ybT[0][:, h, :], at[:], subln[0][:], 1.0 - LAM_INIT, ALU.mult, ALU.mult, [atb, subln[1]], [ybTb[h]])
                    pending.append(head_epilogue)
                while pending:
                    pending.pop(0)()
                for tt in range(4):
                    rt_, rb = slots[tt]
                    for dh in range(2):
                        pd, pdb = B.bank()
                        for c in range(8):
                            if c < 4:
                                lh, lb = yg[:, c, tt * 128:(tt + 1) * 128], ygb
                            else:
                                lh, lb = ybT[0][:, c - 4, tt * 128:(tt + 1) * 128], ybTb[c - 4]
                            B.mm(pd[:], pdb, lh, wout[0][:, c, dh * 512:(dh + 1) * 512], [lb, woutb[c // 2]], start=(c == 0), stop=(c == 7))
                        B.stt(rt_[:, dh * 512:(dh + 1) * 512], rt_[:, dh * 512:(dh + 1) * 512], ALPHA, pd[:], ALU.mult, ALU.add, [pdb, rb], [rb])
                    k2 = (g * 4 + tt) % 2
                    layer_norm(B, rt_, rb, stt_[k2], mv[k2], sd[k2], rs[k2], gB, bB)
                    t0 = g0 + tt * 128
                    P.dma("sp", x2_d[t0:t0 + 128, :], rt_[:], [rb], [], rb)
                if ple_d is not None:
                    for tt in range(4):
                        rt_, rb = slots[tt]
                        t0 = g0 + tt * 128
                        k2 = (g * 4 + tt) % 2
                        xT2, xT2b = x2T[k2]
                        for half in range(2):
                            pt, pb = B.bank()
                            for cc_ in range(4):
                                c = half * 4 + cc_
                                B.tr(pt[:, cc_ * 128:(cc_ + 1) * 128], pb, rt_[:, c * 128:(c + 1) * 128], [rb], inc=(cc_ == 3))
                            B.cp("act" if half == 0 else "dve", xT2[:, half * 4:half * 4 + 4, :], r3(pt[:]), [pb], [xT2b])
                        ptl, ptlb = ptile[k2]
                        pTx, pTxb = pTt[k2]
                        P.dma("sp", ptl[:], p_d[t0:t0 + 128, :], [], [ptlb], ptlb)
                        pt, pb = B.bank()
                        for c in range(2):
                            B.tr(pt[:, c * 128:(c + 1) * 128], pb, ptl[:, c * 128:(c + 1) * 128], [ptlb], inc=(c == 1))
                        B.cp("act", pTx[:], r3(pt[:, 0:256], 2), [pb], [pTxb])
                        plt, plb = plebuf[k2]
                        for dh in range(2):
                            pgt, pgb = B.bank()
                            for c in range(8):
                                B.mm(pgt[:], pgb, xT2[:, c, :], wg[0][:, c, dh * 512:(dh + 1) * 512], [xT2b, wgb[c // 2]], start=(c == 0), stop=(c == 7))
                            ppt, ppb = B.bank()
                            for c in range(2):
                                B.mm(ppt[:], ppb, pTx[:, c, :], wp[0][:, c, dh * 512:(dh + 1) * 512], [pTxb, wgb[4]], start=(c == 0), stop=(c == 1))
                            B.act(sg[0][:], pgt[:], AF.Sigmoid, [pgb], [sg[1]])
                            B.tt("dve", plt[:, dh * 512:(dh + 1) * 512], sg[0][:], ppt[:], ALU.mult, [sg[1], ppb], [plb])
                        P.dma("sp", ple_d[t0:t0 + 128, :], plt[:], [plb], [], plb)
        allb = [x[1] for x in [KT, V, wout, cst, tri, lpar, subln, gB, bB, ybT, rl, Om, sqa, sdn, eps6]]
        allb += [x[1] for x in qz[0] + qz[1] + yaTg + r + PT + [onesb] + att + stt_ + mv + sd + rs] + KTb + Vb + woutb + ybTb
        allb += [x[1] for x in lam.values()] + [x[1] for x in lsm.values()] + [x[1] for x in B.pbanks]
        if ple_d is not None:
            allb += [x[1] for x in extra] + extra_b
        for en in ("pe", "act", "dve", "pool", "sp"):
            P.wait_all(en, allb)
        P.emit()
```
